# Optimizing a Trainium2 kernel written in Bass

```python
import math
import jax, jax.numpy as jnp
from jax import lax
import numpy as np

D_MODEL = 4096
BATCH = 2
SEQ = 8192
DEPTH = 1
DEC_BATCH = 8
DEC_SEQ = 16
PAST_LEN = 2048

CHUNK = 64
MIX_W = D_MODEL
DN_HEADS = 16
DN_DK = 128
DN_DV = 128
DN_QK_W = DN_HEADS * DN_DK
DN_V_W = DN_HEADS * DN_DV
DN_QKV_W = 2 * DN_QK_W + DN_V_W
CONV_W = 4
RW_HEAD = 64
RW_W = MIX_W - DN_V_W
RW_HEADS = RW_W // RW_HEAD
RW_LORA_W = 96
RW_LORA_A = 96
RW_LORA_G = 256
RW_SHIFT_W = 3 * RW_W + RW_LORA_W + RW_LORA_A + RW_LORA_G
IN_W = DN_QKV_W + DN_V_W + 2 * DN_HEADS + RW_SHIFT_W
D_FF = -(-8 * D_MODEL // (3 * 256)) * 256
NORM_EPS = 1e-6
RW_GN_EPS = 64e-5

kernel_name = 'hymba_gdn_rwkv7_streaming_step'


def rms_norm(x, w, eps=NORM_EPS):
    x32 = x.astype(jnp.float32)
    y = x32 * lax.rsqrt(jnp.mean(x32 * x32, axis=-1, keepdims=True) + eps)
    return (y * w.astype(jnp.float32)).astype(x.dtype)


def l2_normalize(x, eps=1e-6):
    x32 = x.astype(jnp.float32)
    return x32 * lax.rsqrt(jnp.sum(x32 * x32, axis=-1, keepdims=True) + eps)


def causal_conv(u, buf, w):
    t = u.shape[1]
    xp = jnp.concatenate([buf.astype(u.dtype), u], axis=1)
    out = xp[:, 0:t] * w[0]
    for j in range(1, CONV_W):
        out = out + xp[:, j:j + t] * w[j]
    return out, xp[:, t:]


def token_shift(u, buf, mu):
    t = u.shape[1]
    prev = jnp.concatenate([buf.astype(u.dtype), u], axis=1)[:, :t]
    return u + mu * (prev - u), u[:, t - 1:]


def gated_delta_rule(q, k, v, g, beta, s0):
    b, t, h, _ = q.shape
    dv = v.shape[-1]
    n = -(-t // CHUNK)
    pad = n * CHUNK - t

    def blocks(a):
        a = jnp.pad(a, [(0, 0), (0, pad)] + [(0, 0)] * (a.ndim - 2))
        a = a.reshape((b, n, CHUNK) + a.shape[2:])
        return jnp.moveaxis(jnp.moveaxis(a, 3, 2), 1, 0)

    q, k, v, g, beta = blocks(q), blocks(k), blocks(v), blocks(g), blocks(beta)
    gc = jnp.cumsum(g, axis=-1)
    idx = jnp.arange(CHUNK)
    causal = idx[:, None] >= idx[None, :]
    strict = idx[:, None] > idx[None, :]
    decay = jnp.exp(jnp.where(causal, gc[..., :, None] - gc[..., None, :], -jnp.inf))
    kb = k * beta[..., None]
    lower = jnp.where(strict, jnp.einsum('nbhik,nbhjk->nbhij', kb, k) * decay, 0.0)
    eye = jnp.eye(CHUNK, dtype=q.dtype)
    rhs = jnp.concatenate([v * beta[..., None], kb * jnp.exp(gc)[..., None]], axis=-1)
    sol = lax.linalg.triangular_solve(eye + lower, rhs, left_side=True, lower=True, unit_diagonal=True)
    u, wk = sol[..., :dv], sol[..., dv:]
    qk = jnp.einsum('nbhik,nbhjk->nbhij', q, k) * decay
    q_dec = q * jnp.exp(gc)[..., None]
    k_dec = k * jnp.exp(gc[..., -1:] - gc)[..., None]
    g_last = jnp.exp(gc[..., -1])

    def step(S, inp):
        u_c, w_c, qd_c, qk_c, kd_c, gl_c = inp
        v_new = u_c - jnp.einsum('bhck,bhkv->bhcv', w_c, S)
        o = jnp.einsum('bhck,bhkv->bhcv', qd_c, S) + jnp.einsum('bhij,bhjv->bhiv', qk_c, v_new)
        S = S * gl_c[..., None, None] + jnp.einsum('bhck,bhcv->bhkv', kd_c, v_new)
        return S, o

    S, o = lax.scan(step, s0, (u, wk, q_dec, qk, k_dec, g_last))
    o = jnp.moveaxis(jnp.moveaxis(o, 0, 1), 2, 3).reshape(b, n * CHUNK, h, dv)[:, :t]
    return o, S


def rwkv7_recurrence(r, w, k, v, a_vec, b_vec, s0):
    def step(S, inp):
        r_t, w_t, k_t, v_t, a_t, b_t = inp
        sa = jnp.einsum('bhvk,bhk->bhv', S, a_t)
        S = S * w_t[:, :, None, :] + sa[..., None] * b_t[:, :, None, :] + v_t[..., None] * k_t[:, :, None, :]
        return S, jnp.einsum('bhvk,bhk->bhv', S, r_t)

    xs = (jnp.moveaxis(r, 1, 0), jnp.moveaxis(w, 1, 0), jnp.moveaxis(k, 1, 0),
          jnp.moveaxis(v, 1, 0), jnp.moveaxis(a_vec, 1, 0), jnp.moveaxis(b_vec, 1, 0))
    S, ys = lax.scan(step, s0, xs)
    return jnp.moveaxis(ys, 0, 1), S


def layer(x, dn_state, dn_conv, rw_state, rw_shift, params):
    (g_mix_pre, g_mix_post, w_in, dn_conv_w, dn_a_log, dn_dt_bias, dn_norm_w,
     rw_mu, rw_w0, rw_w2, rw_a0, rw_a2, rw_g2, rw_k_k, rw_k_a, rw_r_k, rw_ln_w, rw_ln_b,
     w_out, g_ffn_pre, g_ffn_post, w_gate, w_up, w_down) = params
    f32 = jnp.float32
    b, t, _ = x.shape
    h = rms_norm(x, g_mix_pre)
    proj = h @ w_in
    o1 = DN_QKV_W
    o2 = o1 + DN_V_W
    o3 = o2 + DN_HEADS
    o4 = o3 + DN_HEADS
    dn_qkv, dn_z, dn_b, dn_a, rw_in = proj[..., :o1], proj[..., o1:o2], proj[..., o2:o3], proj[..., o3:o4], proj[..., o4:]

    qkv, new_dn_conv = causal_conv(dn_qkv, dn_conv, dn_conv_w)
    qkv = jax.nn.silu(qkv)
    q = l2_normalize(qkv[..., :DN_QK_W].reshape(b, t, DN_HEADS, DN_DK)) * (DN_DK ** -0.5)
    k = l2_normalize(qkv[..., DN_QK_W:2 * DN_QK_W].reshape(b, t, DN_HEADS, DN_DK))
    v = qkv[..., 2 * DN_QK_W:].reshape(b, t, DN_HEADS, DN_DV).astype(f32)
    beta = jax.nn.sigmoid(dn_b.astype(f32))
    g = -jnp.exp(dn_a_log.astype(f32)) * jax.nn.softplus(dn_a.astype(f32) + dn_dt_bias.astype(f32))
    o, new_dn_state = gated_delta_rule(q, k, v, g, beta, dn_state.astype(f32))
    o = rms_norm(o, dn_norm_w) * jax.nn.silu(dn_z.reshape(b, t, DN_HEADS, DN_DV).astype(f32))
    y_a = o.reshape(b, t, DN_V_W)

    rw, new_rw_shift = token_shift(rw_in, rw_shift, rw_mu)
    rw = rw.astype(f32)
    s1, s2, s3 = RW_W, 2 * RW_W, 3 * RW_W
    s4, s5 = s3 + RW_LORA_W, s3 + RW_LORA_W + RW_LORA_A
    r, kr, vr = rw[..., :s1], rw[..., s1:s2], rw[..., s2:s3]
    xw, xa, xg = rw[..., s3:s4], rw[..., s4:s5], rw[..., s5:]
    w_log = -jax.nn.softplus(-(rw_w0 + jnp.tanh(xw) @ rw_w2)) - 0.5
    w_dec = jnp.exp(-jnp.exp(w_log))
    a = jax.nn.sigmoid(rw_a0 + xa @ rw_a2)
    gate = jax.nn.sigmoid(xg) @ rw_g2
    heads = lambda z: z.reshape(b, t, RW_HEADS, RW_HEAD)
    kk = l2_normalize(heads(kr * rw_k_k))
    kr = kr * (1.0 + (a - 1.0) * rw_k_a)
    r_h, k_h, v_h, a_h = heads(r), heads(kr), heads(vr), heads(a)
    y, new_rw_state = rwkv7_recurrence(r_h, heads(w_dec), k_h, v_h, -kk, kk * a_h, rw_state.astype(f32))
    mean = jnp.mean(y, axis=-1, keepdims=True)
    var = jnp.mean(jnp.square(y - mean), axis=-1, keepdims=True)
    y = ((y - mean) * lax.rsqrt(var + RW_GN_EPS)).reshape(b, t, RW_W) * rw_ln_w + rw_ln_b
    bonus = jnp.sum(r_h * k_h * rw_r_k, axis=-1, keepdims=True) * v_h
    y_b = (y + bonus.reshape(b, t, RW_W)) * gate

    mix = jnp.concatenate([y_a, y_b], axis=-1).astype(x.dtype)
    x = x + rms_norm(mix @ w_out, g_mix_post)

    h2 = rms_norm(x, g_ffn_pre)
    f = (jax.nn.silu(h2 @ w_gate) * (h2 @ w_up)) @ w_down
    x = x + rms_norm(f, g_ffn_post)
    return x, (new_dn_state.astype(dn_state.dtype), new_dn_conv.astype(dn_conv.dtype),
               new_rw_state.astype(rw_state.dtype), new_rw_shift.astype(rw_shift.dtype))


def setup_inputs(seed: int = 0) -> dict:
    key = jax.random.key(seed)
    ks = jax.random.split(key, 32)
    f32 = jnp.float32
    L = DEPTH

    def nrm(k, shape, scale):
        return jax.random.normal(k, shape, f32) * scale

    dt = jnp.exp(jax.random.uniform(ks[11], (L, DN_HEADS), f32, math.log(1e-3), math.log(1e-1)))
    return {
        'x_prompt': nrm(ks[0], (BATCH, SEQ, D_MODEL), 1.0),
        'x_sample': nrm(ks[1], (DEC_BATCH, DEC_SEQ, D_MODEL), 1.0),
        'state_dn': nrm(ks[2], (L, DEC_BATCH, DN_HEADS, DN_DK, DN_DV), 0.1),
        'cache_dn_conv': nrm(ks[3], (L, DEC_BATCH, CONV_W - 1, DN_QKV_W), 1.0),
        'state_rwkv': nrm(ks[4], (L, DEC_BATCH, RW_HEADS, RW_HEAD, RW_HEAD), 0.1),
        'cache_rwkv_shift': nrm(ks[5], (L, DEC_BATCH, 1, RW_SHIFT_W), 1.0),
        'g_mix_pre': 1.0 + nrm(ks[6], (L, D_MODEL), 0.05),
        'g_mix_post': 1.0 + nrm(ks[7], (L, D_MODEL), 0.05),
        'w_in': nrm(ks[8], (L, D_MODEL, IN_W), D_MODEL ** -0.5),
        'dn_conv_w': nrm(ks[9], (L, CONV_W, DN_QKV_W), CONV_W ** -0.5),
        'dn_a_log': jnp.log(jax.random.uniform(ks[10], (L, DN_HEADS), f32, 1.0, 16.0)),
        'dn_dt_bias': dt + jnp.log(-jnp.expm1(-dt)),
        'dn_norm_w': 1.0 + nrm(ks[12], (L, DN_DV), 0.05),
        'rw_mu': jax.random.uniform(ks[13], (L, RW_SHIFT_W), f32, 0.0, 1.0),
        'rw_w0': jax.random.uniform(ks[14], (L, RW_W), f32, -6.0, 1.0),
        'rw_w2': nrm(ks[15], (L, RW_LORA_W, RW_W), 0.5 * RW_LORA_W ** -0.5),
        'rw_a0': nrm(ks[16], (L, RW_W), 0.5),
        'rw_a2': nrm(ks[17], (L, RW_LORA_A, RW_W), RW_LORA_A ** -0.5),
        'rw_g2': nrm(ks[18], (L, RW_LORA_G, RW_W), RW_LORA_G ** -0.5),
        'rw_k_k': 0.85 + nrm(ks[19], (L, RW_W), 0.05),
        'rw_k_a': 1.0 + nrm(ks[20], (L, RW_W), 0.05),
        'rw_r_k': nrm(ks[21], (L, RW_HEADS, RW_HEAD), 0.1),
        'rw_ln_w': 1.0 + nrm(ks[22], (L, RW_W), 0.05),
        'rw_ln_b': nrm(ks[23], (L, RW_W), 0.02),
        'w_out': nrm(ks[24], (L, MIX_W, D_MODEL), MIX_W ** -0.5),
        'g_ffn_pre': 1.0 + nrm(ks[25], (L, D_MODEL), 0.05),
        'g_ffn_post': 1.0 + nrm(ks[26], (L, D_MODEL), 0.05),
        'w_gate': nrm(ks[27], (L, D_MODEL, D_FF), D_MODEL ** -0.5),
        'w_up': nrm(ks[28], (L, D_MODEL, D_FF), D_MODEL ** -0.5),
        'w_down': nrm(ks[29], (L, D_FF, D_MODEL), D_FF ** -0.5),
    }


def reference(x_prompt, x_sample, state_dn, cache_dn_conv, state_rwkv, cache_rwkv_shift,
              g_mix_pre, g_mix_post, w_in, dn_conv_w, dn_a_log, dn_dt_bias, dn_norm_w,
              rw_mu, rw_w0, rw_w2, rw_a0, rw_a2, rw_g2, rw_k_k, rw_k_a, rw_r_k, rw_ln_w, rw_ln_b,
              w_out, g_ffn_pre, g_ffn_post, w_gate, w_up, w_down):
    bp = x_prompt.shape[0]
    dt = x_prompt.dtype
    yp, ys = x_prompt, x_sample
    p_dn, p_conv, p_rw, p_shift = [], [], [], []
    s_dn, s_conv, s_rw, s_shift = [], [], [], []
    for l in range(DEPTH):
        params = (g_mix_pre[l], g_mix_post[l], w_in[l], dn_conv_w[l], dn_a_log[l], dn_dt_bias[l], dn_norm_w[l],
                  rw_mu[l], rw_w0[l], rw_w2[l], rw_a0[l], rw_a2[l], rw_g2[l], rw_k_k[l], rw_k_a[l], rw_r_k[l],
                  rw_ln_w[l], rw_ln_b[l], w_out[l], g_ffn_pre[l], g_ffn_post[l], w_gate[l], w_up[l], w_down[l])
        yp, (a0, a1, a2, a3) = layer(
            yp,
            jnp.zeros((bp, DN_HEADS, DN_DK, DN_DV), dt),
            jnp.zeros((bp, CONV_W - 1, DN_QKV_W), dt),
            jnp.zeros((bp, RW_HEADS, RW_HEAD, RW_HEAD), dt),
            jnp.zeros((bp, 1, RW_SHIFT_W), dt),
            params)
        p_dn.append(a0); p_conv.append(a1); p_rw.append(a2); p_shift.append(a3)
        ys, (c0, c1, c2, c3) = layer(ys, state_dn[l], cache_dn_conv[l], state_rwkv[l], cache_rwkv_shift[l], params)
        s_dn.append(c0); s_conv.append(c1); s_rw.append(c2); s_shift.append(c3)
    return (yp, ys,
            jnp.stack(p_dn), jnp.stack(p_conv), jnp.stack(p_rw), jnp.stack(p_shift),
            jnp.stack(s_dn), jnp.stack(s_conv), jnp.stack(s_rw), jnp.stack(s_shift))
```

```python
import os
import numpy as np
import contextlib
import concourse.bass as bass
import concourse.mybir as mybir

F32 = mybir.dt.float32
BF16 = mybir.dt.bfloat16
AF = mybir.ActivationFunctionType
ALU = mybir.AluOpType
AX = mybir.AxisListType


class Res:
    def __init__(self, t, name):
        self.t = t
        self.name = name
        self.ws = []
        self.rd = []
        self.disjoint = False
        self.dk = None
        self.dn = 0

    def __getitem__(self, idx):
        return self.t[idx]


class Ctx:
    ENG = ("pe", "act", "dve", "pool", "sp")
    EPOCH = 30000

    def __init__(self, nc):
        self.nc = nc
        self.es = contextlib.ExitStack()
        self.ops = {e: [] for e in self.ENG}
        self.cnt = {e: 0 for e in self.ENG}
        self.known = {e: {} for e in self.ENG}
        self.sems = {}
        self.nres = 0

    def _sem(self, key):
        if key not in self.sems:
            self.sems[key] = self.es.enter_context(self.nc.semaphore("s%d" % len(self.sems)))
        return key

    def push(self):
        if not hasattr(self, "stack"):
            self.stack = []
        self.stack.append((contextlib.ExitStack(), []))

    def pop(self):
        es, rl = self.stack.pop()
        self.barrier(rl)
        es.close()

    def _alloc(self, cm, name):
        if getattr(self, "stack", None):
            es, rl = self.stack[-1]
            t = es.enter_context(cm)
            r = Res(t, name)
            rl.append(r)
            return r
        return Res(self.es.enter_context(cm), name)

    def sb(self, shape, dt=F32, name=None):
        self.nres += 1
        name = (name or "t") + ("_%d" % self.nres)
        return self._alloc(self.nc.sbuf_tensor(name, list(shape), dt), name)

    def ps(self, shape, dt=F32, name=None):
        self.nres += 1
        name = (name or "p") + ("_%d" % self.nres)
        return self._alloc(self.nc.psum_tensor(name, list(shape), dt), name)

    def dram(self, name, shape, dt, kind="Internal"):
        t = self.nc.dram_tensor(name, list(shape), dt, kind=kind)
        r = Res(t, name)
        r.disjoint = True
        return r

    def barrier(self, res_list):
        for e in self.ENG:
            toks = []
            for r in res_list:
                toks.extend(r.ws); toks.extend(r.rd)
            best = {}
            for k, v in toks:
                if v > best.get(k, 0):
                    best[k] = v
            out = []
            kn = self.known[e]
            for k, v in best.items():
                if kn.get(k, 0) < v:
                    kn[k] = v
                    out.append((k, v))
            if out:
                self.ops[e].append((out, None, None, 0))

    def _waits(self, eng, reads, writes, acc=False):
        toks = []
        for r in reads:
            toks.extend(r.ws)
        for w in writes:
            if w.disjoint:
                continue
            for t in w.ws:
                if not (acc and eng == "pe" and t[0].startswith("E_pe")):
                    toks.append(t)
            toks.extend(w.rd)
        best = {}
        for k, v in toks:
            if v > best.get(k, 0):
                best[k] = v
        out = []
        kn = self.known[eng]
        for k, v in best.items():
            if kn.get(k, 0) >= v:
                continue
            kn[k] = v
            out.append((k, v))
        return out

    def _commit(self, tok, reads, writes):
        for r in reads:
            if r.disjoint and not r.ws:
                continue
            d = dict(r.rd)
            d[tok[0]] = max(d.get(tok[0], 0), tok[1])
            r.rd = list(d.items())
        for w in writes:
            if w.disjoint:
                d = dict(w.ws)
                d[tok[0]] = max(d.get(tok[0], 0), tok[1])
                w.ws = list(d.items())
            else:
                w.ws = [tok]
                w.rd = []

    def op(self, eng, fn, reads=(), writes=(), acc=False):
        waits = self._waits(eng, reads, writes, acc)
        self.cnt[eng] += 1
        n = self.cnt[eng]
        ep = (n - 1) // self.EPOCH
        key = self._sem("E_%s_%d" % (eng, ep))
        self.ops[eng].append((waits, fn, key, 1))
        self._commit((key, n - ep * self.EPOCH), reads, writes)

    def dma(self, eng, out_ap, in_ap, reads=(), writes=(), semres=None, **kw):
        waits = self._waits(eng, reads, writes)
        sr = semres
        if sr.dk is None:
            sr.dk = self._sem("D_" + sr.name)
        sr.dn += 16
        tok = (sr.dk, sr.dn)
        fn = lambda e, o=out_ap, i=in_ap, kw=kw: e.dma_start(out=o, in_=i, **kw)
        self.ops[eng].append((waits, fn, sr.dk, 16))
        self._commit(tok, reads, writes)
        return tok

    def final_wait(self, eng, res_list):
        waits = self._waits(eng, res_list, [])
        self.ops[eng].append((waits, None, None, 0))

    def emit(self):
        nc = self.nc
        engobj = {"pe": "tensor", "act": "scalar", "dve": "vector", "pool": "gpsimd", "sp": "sync"}
        with nc.Block() as block:
            for e in self.ENG:
                ops = self.ops[e]
                sems = self.sems

                def body(eng, ops=ops):
                    for waits, fn, key, inc in ops:
                        for k, v in waits:
                            eng.wait_ge(sems[k], v)
                        if fn is not None:
                            fn(eng).then_inc(sems[key], inc)
                getattr(block, engobj[e])(body)

    def close(self):
        self.es.close()


class Banks:
    def __init__(self, c, n, dt=F32, w=512):
        self.b = [c.ps([128, w], dt) for _ in range(n)]
        self.i = 0

    def get(self):
        r = self.b[self.i % len(self.b)]
        self.i += 1
        return r


def rsqrt_ops(c, out, src, srcres, scale, eps_ap, consts, tmp, post_bias=None):
    c.op("act", lambda e: e.activation(tmp[:], src, AF.Ln, bias=eps_ap, scale=scale), reads=[srcres, consts.res], writes=[tmp])
    if post_bias is None:
        c.op("act", lambda e: e.activation(out[:], tmp[:], AF.Exp, scale=-0.5), reads=[tmp], writes=[out])
    else:
        c.op("act", lambda e: e.activation(out[:], tmp[:], AF.Exp, scale=-0.5, bias=post_bias), reads=[tmp, consts.res], writes=[out])


def build_hT(c, xall, hT, ntiles, consts):
    xt = [c.sb([128, 4096], F32) for _ in range(2)]
    xn = [c.sb([128, 4096], BF16) for _ in range(2)]
    hs = [c.sb([128, 32, 128], BF16) for _ in range(2)]
    st = [c.sb([128, 4], F32) for _ in range(2)]
    pb = Banks(c, 4, BF16, 1024)
    for i in range(ntiles):
        x_, n_, h_, s_ = xt[i % 2], xn[i % 2], hs[i % 2], st[i % 2]
        c.dma("sp", x_[:], xall[i * 128:(i + 1) * 128, :], reads=[xall], writes=[x_], semres=x_)
        c.op("act", lambda e, x_=x_, n_=n_, s_=s_: e.activation(n_[:], x_[:], AF.Square, accum_out=s_[:, 0:1]), reads=[x_], writes=[n_, s_])
        c.op("act", lambda e, s_=s_: e.activation(s_[:, 1:2], s_[:, 0:1], AF.Ln, bias=consts[:, 0:1], scale=1.0 / 4096), reads=[s_, consts.res], writes=[s_])
        c.op("act", lambda e, s_=s_: e.activation(s_[:, 2:3], s_[:, 1:2], AF.Exp, scale=-0.5), reads=[s_], writes=[s_])
        c.op("dve", lambda e, x_=x_, n_=n_, s_=s_: e.tensor_scalar(n_[:], x_[:], s_[:, 2:3], None, ALU.mult), reads=[x_, s_], writes=[n_])
        for g in range(4):
            p = pb.get()
            for j in range(8):
                k = g * 8 + j
                c.op("pe", lambda e, p=p, j=j, k=k, n_=n_: e.transpose(p[:, j * 128:(j + 1) * 128], n_[:, k * 128:(k + 1) * 128], consts.identb[:]),
                     reads=[n_, consts.identb_res], writes=[p], acc=True)
            eng = "act" if g % 2 == 0 else "dve"
            if eng == "act":
                c.op("act", lambda e, p=p, g=g, h_=h_: e.activation(h_[:, g * 8:(g + 1) * 8, :], p[:].rearrange("p (a b) -> p a b", b=128), AF.Copy), reads=[p], writes=[h_])
            else:
                c.op("dve", lambda e, p=p, g=g, h_=h_: e.tensor_copy(h_[:, g * 8:(g + 1) * 8, :], p[:].rearrange("p (a b) -> p a b", b=128)), reads=[p], writes=[h_])
        c.dma("sp", hT[i], h_[:], reads=[h_], writes=[hT], semres=h_)


class Consts:
    pass


def load_consts(c, cdram):
    k = Consts()
    t = c.sb([128, 1024], F32, name="consts")
    c.dma("sp", t[:], cdram[:, :], reads=[cdram], writes=[t], semres=t)
    k.t = t
    k.res = t
    tb = c.sb([128, 128], BF16, name="identb")
    c.op("dve", lambda e: e.tensor_copy(tb[:], t[:, 128:256]), reads=[t], writes=[tb])
    k.identb = tb
    k.identb_res = tb
    return k


def _getitem(self, idx):
    return self.t[idx]


Consts.__getitem__ = _getitem


def make_consts_np():
    a = np.zeros((128, 1024), np.float32)
    a[:, 0] = 1e-6
    a[:, 1] = 1.0
    a[:, 2] = np.log(128.0 ** -0.5)
    a[:, 3] = 64e-5
    a[:, 4] = -0.5
    a[0:64, 640:704] = 1.0
    a[64:128, 704:768] = 1.0
    a[:, 128:256] = np.eye(128)
    a[:, 256:384] = 1.0
    a[:, 384:512] = np.triu(np.ones((128, 128)), 0)
    a[:, 512:640] = np.triu(np.ones((128, 128)), 1)
    return a


def inv_unit_upper(c, pb, M, C, tmp, K):
    ident = K.t
    p = pb.get()
    c.op("pe", lambda e: e.transpose(p[0:C, 0:C], M[0:C, 0:C], ident[0:C, 128:128 + C]), reads=[M, K.res], writes=[p])
    P, Q = tmp["P"], tmp["Q"]
    R = tmp["R"]
    c.op("act", lambda e: e.activation(Q[0][0:C, 0:C], p[0:C, 0:C], AF.Copy), reads=[p], writes=[Q[0]])
    c.op("dve", lambda e: e.tensor_tensor(R[0][0:C, 0:C], ident[0:C, 128:128 + C], M[0:C, 0:C], ALU.subtract), reads=[K.res, M], writes=[R[0]])
    Pc, Qc, Rc = M, Q[0], R[0]
    nlev = int(np.log2(C)) - 1
    for l in range(nlev):
        Pn, Qn, Rn = P[l % 2], Q[(l + 1) % 2], R[(l + 1) % 2]
        p1 = pb.get()
        c.op("pe", lambda e, p1=p1, Qc=Qc, Pc=Pc: e.matmul(p1[0:C, 0:C], Qc[0:C, 0:C], Pc[0:C, 0:C], start=True, stop=True), reads=[Qc, Pc], writes=[p1])
        last = (l == nlev - 1)
        p2 = pb.get()
        c.op("pe", lambda e, p2=p2, Qc=Qc, Pc=Pc: e.matmul(p2[0:C, 0:C], Pc[0:C, 0:C], Qc[0:C, 0:C], start=True, stop=True), reads=[Qc, Pc], writes=[p2])
        if not last:
            c.op("dve", lambda e, p1=p1, Pn=Pn: e.tensor_copy(Pn[0:C, 0:C], p1[0:C, 0:C]), reads=[p1], writes=[Pn])
        c.op("act", lambda e, p2=p2, Qn=Qn: e.activation(Qn[0:C, 0:C], p2[0:C, 0:C], AF.Copy), reads=[p2], writes=[Qn])
        p3 = pb.get()
        c.op("pe", lambda e, p3=p3, Qn=Qn, Rc=Rc: e.matmul(p3[0:C, 0:C], Qn[0:C, 0:C], Rc[0:C, 0:C], start=True, stop=True), reads=[Qn, Rc], writes=[p3])
        c.op("dve", lambda e, p3=p3, Rn=Rn, Rc=Rc: e.tensor_tensor(Rn[0:C, 0:C], p3[0:C, 0:C], Rc[0:C, 0:C], ALU.add), reads=[p3, Rc], writes=[Rn])
        Pc, Qc, Rc = Pn, Qn, Rn
    return Rc


def dn_phase(c, K, hT, wdn, gpre, cwd, ccache, alog, dtb, normw, sdn_in, seqs, outs, SEQ):
    NC = 1028
    wb = c.sb([128, 32, NC], BF16, name="wb_dn")
    gp = c.sb([128, 32], F32, name="gpre")
    c.dma("sp", gp[:], gpre[:, :], reads=[gpre], writes=[gp], semres=gp)
    c.push()
    stg = [c.sb([128, NC], F32) for _ in range(2)]
    for k in range(32):
        s_ = stg[k % 2]
        c.dma("sp", s_[:], wdn[k * 128:(k + 1) * 128, :], reads=[wdn], writes=[s_], semres=s_)
        c.op("dve", lambda e, s_=s_, k=k: e.tensor_scalar(wb[:, k, :], s_[:], gp[:, k:k + 1], None, ALU.mult), reads=[s_, gp], writes=[wb], acc=True)
    c.pop()
    cw = c.sb([128, 6, 4], F32, name="convw")
    c.dma("sp", cw[:], cwd[:, :, :], reads=[cwd], writes=[cw], semres=cw)
    gpar = c.sb([4, 4], F32, name="gpar")
    c.dma("sp", gpar[:, 0:1], dtb[:, :], reads=[dtb], writes=[gpar], semres=gpar)
    c.dma("sp", gpar[:, 1:2], alog[:, :], reads=[alog], writes=[gpar], semres=gpar)
    c.op("act", lambda e: e.activation(gpar[:, 2:3], gpar[:, 1:2], AF.Exp), reads=[gpar], writes=[gpar])
    c.op("dve", lambda e: e.tensor_scalar(gpar[:, 3:4], gpar[:, 2:3], -1.0, None, ALU.mult), reads=[gpar], writes=[gpar])
    nwb = c.sb([128, 128], F32, name="normw")
    c.dma("sp", nwb[:], normw[:, :], reads=[normw], writes=[nwb], semres=nwb)

    pb = Banks(c, 8)
    ht = [c.sb([128, 4, 32, 128], BF16) for _ in range(1)]
    pj = [c.sb([128, 3 + 512], F32, name="pj%d" % m) for m in range(6)]
    acc = [c.sb([128, 512], F32) for _ in range(2)]
    qkv = [c.sb([128, 512], F32, name="qkv%d" % m) for m in range(6)]
    zs = [c.sb([128, 512], F32, name="zs%d" % m) for m in range(2)]
    sq = [c.sb([128, 512], F32) for _ in range(2)]
    lnt = [c.sb([128, 512], F32) for _ in range(2)]
    rst = [c.sb([128, 512], F32) for _ in range(2)]
    g4 = c.sb([4, 512], F32, name="g4"); sg4 = c.sb([4, 512], F32, name="sg4"); e4 = c.sb([4, 512], F32, name="e4")
    S = [[c.sb([128, 128], F32, name="S%d_%d" % (h, i)) for i in range(2)] for h in range(2)]
    NSLOT = 2
    slots = []
    for s in range(NSLOT):
        d = {}
        for nm in ["gb", "tt", "dect", "decs", "decm", "M", "A"]:
            d[nm] = c.sb([64, 64], F32)
        d["P"] = [c.sb([64, 64], F32) for _ in range(2)]
        d["Q"] = [c.sb([64, 64], F32) for _ in range(2)]
        d["R"] = [c.sb([64, 64], F32) for _ in range(2)]
        d["gbt"] = c.sb([64, 8], F32)
        d["gcs"] = c.sb([128, 8], F32)
        for nm in ["ktok", "vtok", "ztok", "kd", "X2", "vnew", "av", "o", "pre", "res", "junk"]:
            d[nm] = c.sb([64, 128], F32)
        d["st"] = c.sb([64, 4], F32)
        slots.append(d)
    state = {"slot_i": 0}

    def do_seq(si, tok0, T, kind, b):
        C = 64 if T >= 64 else T
        ntile = max(1, T // 512)
        TT = min(T, 512)
        for m in range(6):
            if kind == "p":
                c.op("pool", lambda e, m=m: e.memset(pj[m][:, 0:3], 0.0), writes=[pj[m]])
            else:
                c.dma("sp", pj[m][:, 0:3], ccache[:, b, m, :], reads=[ccache], writes=[pj[m]], semres=pj[m])
        Sc = [S[0][0], S[1][0]]
        Sn = [S[0][1], S[1][1]]
        for h in range(2):
            if kind == "p":
                c.op("pool", lambda e, h=h: e.memset(Sc[h][:], 0.0), writes=[Sc[h]])
            else:
                c.dma("sp", Sc[h][:], sdn_in[b, h], reads=[sdn_in], writes=[Sc[h]], semres=Sc[h])
        def do_tile(ti):
            t0 = tok0 + ti * 512
            h_ = ht[0]
            blk0 = t0 // 128
            off = t0 % 128
            if TT == 512:
                c.dma("sp", h_[:], hT[blk0:blk0 + 4].rearrange("a p k t -> p a k t"), reads=[hT], writes=[h_], semres=h_)
                rhs = lambda k, h_=h_: h_[:, :, k, :]
            else:
                c.dma("sp", h_[:, 0], hT[blk0], reads=[hT], writes=[h_], semres=h_)
                rhs = lambda k, h_=h_, off=off: h_[:, 0, k, off:off + TT]
            def do_proj(m):
                p = pb.get()
                for k in range(32):
                    c.op("pe", lambda e, p=p, k=k, m=m, rhs=rhs: e.matmul(p[:, 0:TT], wb[:, k, m * 128:(m + 1) * 128], rhs(k), start=(k == 0), stop=(k == 31)),
                         reads=[wb, h_], writes=[p], acc=True)
                if m < 6:
                    c.op("act", lambda e, p=p, m=m: e.activation(pj[m][:, 3:3 + TT], p[:, 0:TT], AF.Copy), reads=[p], writes=[pj[m]])
                    a_ = acc[m % 2]
                    c.op("dve", lambda e, m=m, a_=a_: e.tensor_scalar(a_[:, 0:TT], pj[m][:, 0:TT], cw[:, m, 0:1], None, ALU.mult), reads=[pj[m], cw], writes=[a_])
                    for j in range(1, 4):
                        c.op("dve", lambda e, m=m, a_=a_, j=j: e.scalar_tensor_tensor(a_[:, 0:TT], pj[m][:, j:j + TT], cw[:, m, j:j + 1], a_[:, 0:TT], ALU.mult, ALU.add),
                             reads=[pj[m], cw, a_], writes=[a_])
                    c.op("act", lambda e, m=m, a_=a_: e.activation(qkv[m][:, 0:TT], a_[:, 0:TT], AF.Silu), reads=[a_], writes=[qkv[m]])
                    if ti == ntile - 1:
                        c.dma("sp", outs["conv"][si, m], pj[m][:, TT:TT + 3], reads=[pj[m]], writes=[outs["conv"]], semres=pj[m])
                    c.op("pool", lambda e, m=m: e.tensor_copy(pj[m][:, 0:3], pj[m][:, TT:TT + 3]), reads=[pj[m]], writes=[pj[m]])
                else:
                    c.op("act", lambda e, p=p, m=m: e.activation(zs[m - 6][:, 0:TT], p[:, 0:TT], AF.Silu), reads=[p], writes=[zs[m - 6]])
            for m in range(8):
                do_proj(m)
            p = pb.get()
            for k in range(32):
                c.op("pe", lambda e, p=p, k=k, rhs=rhs: e.matmul(p[0:4, 0:TT], wb[:, k, 1024:1028], rhs(k), start=(k == 0), stop=(k == 31)), reads=[wb, h_], writes=[p], acc=True)
            c.op("act", lambda e, p=p: e.activation(sg4[:, 0:TT], p[0:4, 0:TT], AF.Sigmoid), reads=[p], writes=[sg4])
            c.op("act", lambda e, p=p: e.activation(e4[:, 0:TT], p[0:4, 0:TT], AF.Exp, bias=gpar[:, 0:1]), reads=[p, gpar], writes=[e4])
            c.op("act", lambda e: e.activation(e4[:, 0:TT], e4[:, 0:TT], AF.Ln, bias=K[0:4, 1:2]), reads=[e4, K.res], writes=[e4])
            c.op("dve", lambda e: e.tensor_scalar(g4[:, 0:TT], e4[:, 0:TT], gpar[:, 3:4], None, ALU.mult), reads=[e4, gpar], writes=[g4])
            def do_norm(m):
                s_, l_, r_ = sq[m % 2], lnt[m % 2], rst[m % 2]
                c.op("pool", lambda e, m=m, s_=s_: e.tensor_tensor(s_[:, 0:TT], qkv[m][:, 0:TT], qkv[m][:, 0:TT], ALU.mult), reads=[qkv[m]], writes=[s_])
                p = pb.get()
                c.op("pe", lambda e, p=p, s_=s_: e.matmul(p[:, 0:TT], K[:, 256:384], s_[:, 0:TT], start=True, stop=True), reads=[K.res, s_], writes=[p])
                c.op("act", lambda e, p=p, l_=l_: e.activation(l_[:, 0:TT], p[:, 0:TT], AF.Ln, bias=K[:, 0:1]), reads=[p, K.res], writes=[l_])
                if m < 2:
                    c.op("act", lambda e, l_=l_, r_=r_: e.activation(r_[:, 0:TT], l_[:, 0:TT], AF.Exp, scale=-0.5, bias=K[:, 2:3]), reads=[l_, K.res], writes=[r_])
                else:
                    c.op("act", lambda e, l_=l_, r_=r_: e.activation(r_[:, 0:TT], l_[:, 0:TT], AF.Exp, scale=-0.5), reads=[l_], writes=[r_])
                c.op("dve", lambda e, m=m, r_=r_: e.tensor_tensor(qkv[m][:, 0:TT], qkv[m][:, 0:TT], r_[:, 0:TT], ALU.mult), reads=[qkv[m], r_], writes=[qkv[m]])
            for m in range(4):
                do_norm(m)

            def do_ch(ci, h):
                if True:
                    cs = slice(ci * C, (ci + 1) * C)
                    tokc = t0 + ci * C
                    d = slots[state["slot_i"] % NSLOT]
                    state["slot_i"] += 1
                    p = pb.get()
                    c.op("pe", lambda e, p=p, cs=cs: e.transpose(p[0:C, 0:4], sg4[0:4, cs], K[0:4, 128:132]), reads=[sg4, K.res], writes=[p], acc=True)
                    c.op("pe", lambda e, p=p, cs=cs: e.transpose(p[0:C, 4:8], g4[0:4, cs], K[0:4, 128:132]), reads=[g4, K.res], writes=[p], acc=True)
                    gbt = d["gbt"]
                    c.op("act", lambda e, p=p, gbt=gbt: e.activation(gbt[0:C, :], p[0:C, 0:8], AF.Copy), reads=[p], writes=[gbt])
                    beta = lambda gbt=gbt, h=h: gbt[0:C, h:h + 1]
                    gcol = lambda gbt=gbt, h=h: gbt[0:C, 6 + h:7 + h]
                    qT, kT, vT, zT = qkv[h], qkv[2 + h], qkv[4 + h], zs[h]
                    for nm, src in (("ktok", kT), ("vtok", vT), ("ztok", zT)):
                        p = pb.get()
                        c.op("pe", lambda e, p=p, src=src, cs=cs: e.transpose(p[0:C, 0:128], src[:, cs], K[:, 128:256]), reads=[src, K.res], writes=[p])
                        dst = d[nm]
                        eng = "act" if nm != "vtok" else "dve"
                        if eng == "act":
                            c.op("act", lambda e, p=p, dst=dst: e.activation(dst[0:C, :], p[0:C, 0:128], AF.Copy), reads=[p], writes=[dst])
                        else:
                            c.op("dve", lambda e, p=p, dst=dst: e.tensor_copy(dst[0:C, :], p[0:C, 0:128]), reads=[p], writes=[dst])
                    gb = d["gb"]
                    c.op("pool", lambda e, gb=gb, gcol=gcol: e.tensor_scalar(gb[0:C, 0:C], K[0:C, 256:256 + C], gcol(), None, ALU.mult), reads=[K.res, gbt], writes=[gb])
                    pg = pb.get()
                    c.op("pe", lambda e, pg=pg, gb=gb: e.matmul(pg[0:C, 0:C], gb[0:C, 0:C], K[0:C, 384:384 + C], start=True, stop=True), reads=[gb, K.res], writes=[pg])
                    pc = pb.get()
                    c.op("pe", lambda e, pc=pc, gcol=gcol: e.matmul(pc[0:C, 0:1], K[0:C, 384:384 + C], gcol(), start=True, stop=True), reads=[K.res, gbt], writes=[pc], acc=True)
                    c.op("pe", lambda e, pc=pc, gcol=gcol: e.matmul(pc[0:128, 1:2], K[0:C, 256:384], gcol(), start=True, stop=True), reads=[K.res, gbt], writes=[pc], acc=True)
                    gcs = d["gcs"]
                    c.op("dve", lambda e, pc=pc, gcs=gcs: e.tensor_copy(gcs[0:C, 0:1], pc[0:C, 0:1]), reads=[pc], writes=[gcs])
                    c.op("dve", lambda e, pc=pc, gcs=gcs: e.tensor_copy(gcs[:, 1:2], pc[:, 1:2]), reads=[pc], writes=[gcs])
                    c.op("act", lambda e, gcs=gcs: e.activation(gcs[0:C, 2:3], gcs[0:C, 0:1], AF.Exp), reads=[gcs], writes=[gcs])
                    c.op("dve", lambda e, gcs=gcs: e.tensor_scalar(gcs[0:C, 3:4], gcs[0:C, 2:3], -1.0, None, ALU.mult), reads=[gcs], writes=[gcs])
                    c.op("act", lambda e, gcs=gcs: e.activation(gcs[0:C, 4:5], gcs[0:C, 0:1], AF.Exp, scale=-1.0, bias=gcs[0:C, 1:2]), reads=[gcs], writes=[gcs])
                    c.op("act", lambda e, gcs=gcs: e.activation(gcs[:, 5:6], gcs[:, 1:2], AF.Exp), reads=[gcs], writes=[gcs])
                    tt, dect, decs, decm = d["tt"], d["dect"], d["decs"], d["decm"]
                    c.op("dve", lambda e, pg=pg, tt=tt, gcs=gcs: e.tensor_scalar(tt[0:C, 0:C], pg[0:C, 0:C], gcs[0:C, 0:1], 0.0, ALU.subtract, ALU.min), reads=[pg, gcs], writes=[tt])
                    c.op("act", lambda e, tt=tt, dect=dect: e.activation(dect[0:C, 0:C], tt[0:C, 0:C], AF.Exp), reads=[tt], writes=[dect])
                    c.op("pool", lambda e, dect=dect, decs=decs: e.tensor_tensor(decs[0:C, 0:C], dect[0:C, 0:C], K[0:C, 512:512 + C], ALU.mult), reads=[dect, K.res], writes=[decs])
                    c.op("pool", lambda e, dect=dect, decm=decm: e.tensor_tensor(decm[0:C, 0:C], dect[0:C, 0:C], K[0:C, 384:384 + C], ALU.mult), reads=[dect, K.res], writes=[decm])
                    pk = pb.get()
                    c.op("pe", lambda e, pk=pk, kT=kT, cs=cs: e.matmul(pk[0:C, 0:C], kT[:, cs], kT[:, cs], start=True, stop=True), reads=[kT], writes=[pk])
                    M, A = d["M"], d["A"]
                    c.op("dve", lambda e, pk=pk, M=M, beta=beta, decs=decs: e.scalar_tensor_tensor(M[0:C, 0:C], pk[0:C, 0:C], beta(), decs[0:C, 0:C], ALU.mult, ALU.mult), reads=[pk, gbt, decs], writes=[M])
                    pq = pb.get()
                    c.op("pe", lambda e, pq=pq, kT=kT, qT=qT, cs=cs: e.matmul(pq[0:C, 0:C], kT[:, cs], qT[:, cs], start=True, stop=True), reads=[kT, qT], writes=[pq])
                    c.op("dve", lambda e, pq=pq, A=A, decm=decm: e.tensor_tensor(A[0:C, 0:C], pq[0:C, 0:C], decm[0:C, 0:C], ALU.mult), reads=[pq, decm], writes=[A])
                    Rt = inv_unit_upper(c, pb, M, C, d, K)
                    kd = d["kd"]
                    c.op("pool", lambda e, kd=kd, d=d, gcs=gcs: e.tensor_scalar(kd[0:C, :], d["ktok"][0:C, :], gcs[0:C, 4:5], None, ALU.mult), reads=[d["ktok"], gcs], writes=[kd])
                    Sold, Snew = Sc[h], Sn[h]
                    p1 = pb.get()
                    c.op("pe", lambda e, p1=p1, kT=kT, cs=cs, Sold=Sold: e.matmul(p1[0:C, 0:128], kT[:, cs], Sold[:], start=True, stop=True), reads=[kT, Sold], writes=[p1])
                    X2 = d["X2"]
                    c.op("dve", lambda e, p1=p1, X2=X2, d=d, gcs=gcs: e.scalar_tensor_tensor(X2[0:C, :], p1[0:C, 0:128], gcs[0:C, 3:4], d["vtok"][0:C, :], ALU.mult, ALU.add), reads=[p1, gcs, d["vtok"]], writes=[X2])
                    p2 = pb.get()
                    c.op("pe", lambda e, p2=p2, Rt=Rt, X2=X2: e.matmul(p2[0:C, 0:128], Rt[0:C, 0:C], X2[0:C, :], start=True, stop=True), reads=[Rt, X2], writes=[p2])
                    vnew = d["vnew"]
                    c.op("dve", lambda e, p2=p2, vnew=vnew, beta=beta: e.tensor_scalar(vnew[0:C, :], p2[0:C, 0:128], beta(), None, ALU.mult), reads=[p2, gbt], writes=[vnew])
                    p3 = pb.get()
                    c.op("pe", lambda e, p3=p3, kd=kd, vnew=vnew: e.matmul(p3[:, 0:128], kd[0:C, :], vnew[0:C, :], start=True, stop=True), reads=[kd, vnew], writes=[p3])
                    c.op("dve", lambda e, p3=p3, Sold=Sold, Snew=Snew, gcs=gcs: e.scalar_tensor_tensor(Snew[:], Sold[:], gcs[:, 5:6], p3[:, 0:128], ALU.mult, ALU.add), reads=[p3, Sold, gcs], writes=[Snew])
                    p4 = pb.get()
                    c.op("pe", lambda e, p4=p4, A=A, vnew=vnew: e.matmul(p4[0:C, 0:128], A[0:C, 0:C], vnew[0:C, :], start=True, stop=True), reads=[A, vnew], writes=[p4])
                    av = d["av"]
                    c.op("act", lambda e, p4=p4, av=av: e.activation(av[0:C, :], p4[0:C, 0:128], AF.Copy), reads=[p4], writes=[av])
                    p5 = pb.get()
                    c.op("pe", lambda e, p5=p5, qT=qT, cs=cs, Sold=Sold: e.matmul(p5[0:C, 0:128], qT[:, cs], Sold[:], start=True, stop=True), reads=[qT, Sold], writes=[p5])
                    o = d["o"]
                    c.op("dve", lambda e, p5=p5, o=o, av=av, gcs=gcs: e.scalar_tensor_tensor(o[0:C, :], p5[0:C, 0:128], gcs[0:C, 2:3], av[0:C, :], ALU.mult, ALU.add), reads=[p5, gcs, av], writes=[o])
                    Sc[h], Sn[h] = Snew, Sold
                    st = d["st"]
                    c.op("act", lambda e, o=o, d=d, st=st: e.activation(d["junk"][0:C, :], o[0:C, :], AF.Square, accum_out=st[0:C, 0:1]), reads=[o], writes=[d["junk"], st])
                    c.op("act", lambda e, st=st: e.activation(st[0:C, 1:2], st[0:C, 0:1], AF.Ln, bias=K[0:C, 0:1], scale=1.0 / 128), reads=[st, K.res], writes=[st])
                    c.op("act", lambda e, st=st: e.activation(st[0:C, 2:3], st[0:C, 1:2], AF.Exp, scale=-0.5), reads=[st], writes=[st])
                    pre = d["pre"]
                    c.op("pool", lambda e, pre=pre, d=d: e.tensor_tensor(pre[0:C, :], d["ztok"][0:C, :], nwb[0:C, :], ALU.mult), reads=[d["ztok"], nwb], writes=[pre])
                    res = d["res"]
                    c.op("dve", lambda e, res=res, o=o, st=st, pre=pre: e.scalar_tensor_tensor(res[0:C, :], o[0:C, :], st[0:C, 2:3], pre[0:C, :], ALU.mult, ALU.mult), reads=[o, st, pre], writes=[res])
                    c.dma("sp", outs["mix"][tokc:tokc + C, h * 128:(h + 1) * 128], res[0:C, :], reads=[res], writes=[outs["mix"]], semres=res)
            for ci in range(TT // C):
                for h in range(2):
                    do_ch(ci, h)
        for ti in range(ntile):
            do_tile(ti)
        for h in range(2):
            c.dma("sp", outs["sdn"][si, h], Sc[h][:], reads=[Sc[h]], writes=[outs["sdn"]], semres=Sc[h])
        S[0][0], S[0][1] = Sc[0], Sn[0]
        S[1][0], S[1][1] = Sc[1], Sn[1]

    for si, (tok0, T, kind, b) in enumerate(seqs):
        do_seq(si, tok0, T, kind, b)


def rw_phase(c, K, hT, wrw, gpre, P, seqs, outs):
    NC = 1216
    wb = c.sb([128, 32, NC], BF16, name="wb_rw")
    gp = c.sb([128, 32], F32, name="gpre2")
    c.dma("sp", gp[:], gpre[:, :], reads=[gpre], writes=[gp], semres=gp)
    c.push()
    stg = [c.sb([128, NC], F32) for _ in range(2)]
    for k in range(32):
        s_ = stg[k % 2]
        c.dma("sp", s_[:], wrw[k * 128:(k + 1) * 128, :], reads=[wrw], writes=[s_], semres=s_)
        c.op("dve", lambda e, s_=s_, k=k: e.tensor_scalar(wb[:, k, :], s_[:], gp[:, k:k + 1], None, ALU.mult), reads=[s_, gp], writes=[wb])
    c.pop()

    def ld(name, shape):
        t = c.sb(shape, F32, name=name)
        src = P[name]
        c.dma("sp", t[:], src[tuple(slice(None) for _ in shape)], reads=[src], writes=[t], semres=t)
        return t
    mu = ld("mu", [128, 10])
    chp = ld("chp", [128, 2, 4])
    w2 = ld("w2", [96, 256]); a2 = ld("a2", [96, 256]); g2 = ld("g2", [128, 2, 256])
    lnw = ld("lnw", [128, 256]); lnb = ld("lnb", [128, 256]); rkb = ld("rkb", [128, 256])
    shc = ld("shc", [128, 8, 10])
    nchp = c.sb([128, 2, 4], F32, name="nchp")
    c.op("dve", lambda e: e.tensor_scalar(nchp[:], chp[:], -1.0, None, ALU.mult), reads=[chp], writes=[nchp])

    pb = Banks(c, 8)
    TS = 256
    NB = TS // 128
    ht = c.sb([128, NB, 32, 128], BF16, name="ht_rw")
    MC = [(0, 128), (128, 128), (256, 128), (384, 128), (512, 128), (640, 128), (768, 96), (864, 96), (960, 128), (1088, 128)]
    pu = [c.sb([128, 1 + TS], F32, name="pu%d" % m) for m in range(10)]
    rw = [c.sb([128, TS], F32, name="rw%d" % m) for m in range(10)]
    dtmp = [c.sb([128, TS], F32) for _ in range(2)]
    F = {}
    for nm in ["lw", "cwa", "cwb", "E1", "E2", "E3", "E4", "a", "gate", "kk", "kr2", "bv", "at", "bt", "kt", "rt", "Bh", "Kh", "t1", "t2"]:
        if nm in ("at", "bt", "kt", "rt", "Bh", "Kh", "kr2", "gate", "E1"):
            F[nm] = [c.sb([128, TS], F32, name="%s%d" % (nm, j)) for j in range(2)]
        else:
            t_ = c.sb([128, TS], F32, name=nm)
            F[nm] = [t_, t_]
    sgx = [c.sb([128, TS], F32, name="sgx%d" % j) for j in range(2)]
    thw = c.sb([96, TS], F32, name="thw")
    T0t = [c.sb([128, 64], F32, name="T0_%d_%d" % (j, i)) for j in range(2) for i in range(2)]
    T0 = {}
    for j in range(2):
        for i in range(2):
            t = T0t[j * 2 + i]
            T0[(j, i)] = [Res(t.t, t.name + "A"), Res(t.t, t.name + "B")]
    NSLOT = 2
    slots = []
    for s in range(NSLOT):
        d = {}
        for nm in ["M", "Aak", "Arb", "Ark"]:
            d[nm] = c.sb([64, 64], F32)
        d["P"] = [c.sb([64, 64], F32) for _ in range(2)]
        d["Q"] = [c.sb([64, 64], F32) for _ in range(2)]
        d["R"] = [c.sb([64, 64], F32) for _ in range(2)]
        for nm in ["AV", "X", "U", "y2"]:
            d[nm] = c.sb([64, 64], F32)
        d["st"] = c.sb([64, 8], F32)
        d["yc"] = c.sb([64, 64], F32)
        d["junk"] = c.sb([64, 64], F32)
        slots.append(d)
    cslots = []
    for s in range(2):
        d = {}
        for nm in ["vtok", "Bhtok", "Khtok", "rtok", "ktok", "gtok", "Y", "rk", "o1", "o2"]:
            d[nm] = c.sb([64, 256], F32)
        d["rks"] = c.sb([64, 4], F32)
        cslots.append(d)
    state = {"slot": 0, "cslot": 0}
    cur = {0: 0, 1: 0}

    def do_seq(si, tok0, T, kind, b):
        C = 64 if T >= 64 else T
        ntile = max(1, T // TS)
        TT = min(T, TS)
        nch = TT // C
        for m in range(10):
            rows = MC[m][1]
            if kind == "p":
                c.op("pool", lambda e, m=m: e.memset(pu[m][:, 0:1], 0.0), writes=[pu[m]])
            else:
                c.op("pool", lambda e, m=m: e.tensor_copy(pu[m][:, 0:1], shc[:, b, m:m + 1]), reads=[shc], writes=[pu[m]])
        for j in range(2):
            A, B = T0[(j, cur[j])]
            if kind == "p":
                c.op("pool", lambda e, A=A: e.memset(A.t[:], 0.0), writes=[A, B])
            else:
                ld_ = slots[0]["junk"]
                tmp = cslots[0]["o1"]
                c.dma("sp", tmp[:, 0:128], P["srw_in"][b, j], reads=[P["srw_in"]], writes=[tmp], semres=tmp)
                p = pb.get()
                c.op("pe", lambda e, p=p, tmp=tmp: e.transpose(p[:, 0:64], tmp[0:64, 0:128], K[0:64, 128:192]), reads=[tmp, K.res], writes=[p])
                c.op("dve", lambda e, p=p, A=A: e.tensor_copy(A.t[:], p[:, 0:64]), reads=[p], writes=[A, B])

        def do_tile(ti):
            t0 = tok0 + ti * TS
            blk0 = t0 // 128
            off = t0 % 128
            if TT == TS:
                c.dma("sp", ht[:], hT[blk0:blk0 + NB].rearrange("a p k t -> p a k t"), reads=[hT], writes=[ht], semres=ht)
                rhs = lambda k: ht[:, :, k, :]
            else:
                c.dma("sp", ht[:, 0], hT[blk0], reads=[hT], writes=[ht], semres=ht)
                rhs = lambda k: ht[:, 0, k, off:off + TT]

            def do_proj(m):
                col0, rows = MC[m]
                p = pb.get()
                for k in range(32):
                    c.op("pe", lambda e, k=k: e.matmul(p[0:rows, 0:TT], wb[:, k, col0:col0 + rows], rhs(k), start=(k == 0), stop=(k == 31)), reads=[wb, ht], writes=[p], acc=True)
                c.op("act", lambda e: e.activation(pu[m][0:rows, 1:1 + TT], p[0:rows, 0:TT], AF.Copy), reads=[p], writes=[pu[m]])
                d_ = dtmp[m % 2]
                c.op("pool", lambda e: e.tensor_tensor(d_[0:rows, 0:TT], pu[m][0:rows, 0:TT], pu[m][0:rows, 1:1 + TT], ALU.subtract), reads=[pu[m]], writes=[d_])
                c.op("dve", lambda e: e.scalar_tensor_tensor(rw[m][0:rows, 0:TT], d_[0:rows, 0:TT], mu[0:rows, m:m + 1], pu[m][0:rows, 1:1 + TT], ALU.mult, ALU.add), reads=[d_, mu, pu[m]], writes=[rw[m]])
                if ti == ntile - 1:
                    c.dma("sp", outs["shift"][si, col0:col0 + rows].rearrange("(p o) -> p o", o=1), pu[m][0:rows, TT:TT + 1], reads=[pu[m]], writes=[outs["shift"]], semres=pu[m])
                c.op("pool", lambda e: e.tensor_copy(pu[m][0:rows, 0:1], pu[m][0:rows, TT:TT + 1]), reads=[pu[m]], writes=[pu[m]])
            for m in range(10):
                do_proj(m)
            c.op("act", lambda e: e.activation(thw[:, 0:TT], rw[6][0:96, 0:TT], AF.Tanh), reads=[rw[6]], writes=[thw])
            for j in range(2):
                c.op("act", lambda e, j=j: e.activation(sgx[j][:, 0:TT], rw[8 + j][:, 0:TT], AF.Sigmoid), reads=[rw[8 + j]], writes=[sgx[j]])

            def do_pair(j):
                js = slice(j * 128, (j + 1) * 128)
                f = {nm: F[nm][j] for nm in F}
                r_, k_, v_ = rw[j], rw[2 + j], rw[4 + j]
                p = pb.get()
                c.op("pe", lambda e: e.matmul(p[:, 0:TT], w2[:, js], thw[:, 0:TT], start=True, stop=True), reads=[w2, thw], writes=[p])
                t1, t2 = f["t1"], f["t2"]
                c.op("act", lambda e: e.activation(t1[:, 0:TT], p[:, 0:TT], AF.Exp, scale=-1.0, bias=nchp[:, j, 0:1]), reads=[p, nchp], writes=[t1])
                c.op("act", lambda e: e.activation(t1[:, 0:TT], t1[:, 0:TT], AF.Ln, bias=K[:, 1:2]), reads=[t1, K.res], writes=[t1])
                c.op("act", lambda e: e.activation(t2[:, 0:TT], t1[:, 0:TT], AF.Exp, scale=-1.0, bias=K[:, 4:5]), reads=[t1, K.res], writes=[t2])
                lw = f["lw"]
                c.op("dve", lambda e: e.tensor_scalar(lw[:, 0:TT], t2[:, 0:TT], -1.0, None, ALU.mult), reads=[t2], writes=[lw])
                p2 = pb.get()
                c.op("pe", lambda e: e.matmul(p2[:, 0:TT], a2[:, js], rw[7][0:96, 0:TT], start=True, stop=True), reads=[a2, rw[7]], writes=[p2])
                a_ = f["a"]
                c.op("act", lambda e: e.activation(a_[:, 0:TT], p2[:, 0:TT], AF.Sigmoid, bias=chp[:, j, 1:2]), reads=[p2, chp], writes=[a_])
                p3 = pb.get()
                for l in range(2):
                    c.op("pe", lambda e, l=l: e.matmul(p3[:, 0:TT], g2[:, l, js], sgx[l][:, 0:TT], start=(l == 0), stop=(l == 1)), reads=[g2, sgx[l]], writes=[p3], acc=True)
                gate = f["gate"]
                c.op("act", lambda e: e.activation(gate[:, 0:TT], p3[:, 0:TT], AF.Copy), reads=[p3], writes=[gate])
                kk = f["kk"]
                c.op("dve", lambda e: e.tensor_scalar(kk[:, 0:TT], k_[:, 0:TT], chp[:, j, 2:3], None, ALU.mult), reads=[k_, chp], writes=[kk])
                c.op("pool", lambda e: e.tensor_tensor(t1[:, 0:TT], kk[:, 0:TT], kk[:, 0:TT], ALU.mult), reads=[kk], writes=[t1])
                p4 = pb.get()
                c.op("pe", lambda e: e.matmul(p4[:, 0:TT], K[:, 640:768], t1[:, 0:TT], start=True, stop=True), reads=[K.res, t1], writes=[p4])
                c.op("act", lambda e: e.activation(t2[:, 0:TT], p4[:, 0:TT], AF.Ln, bias=K[:, 0:1]), reads=[p4, K.res], writes=[t2])
                c.op("act", lambda e: e.activation(t2[:, 0:TT], t2[:, 0:TT], AF.Exp, scale=-0.5), reads=[t2], writes=[t2])
                c.op("dve", lambda e: e.tensor_tensor(kk[:, 0:TT], kk[:, 0:TT], t2[:, 0:TT], ALU.mult), reads=[kk, t2], writes=[kk])
                kr2 = f["kr2"]
                c.op("dve", lambda e: e.tensor_scalar(t1[:, 0:TT], a_[:, 0:TT], -1.0, chp[:, j, 3:4], ALU.add, ALU.mult), reads=[a_, chp], writes=[t1])
                c.op("dve", lambda e: e.scalar_tensor_tensor(kr2[:, 0:TT], t1[:, 0:TT], 1.0, k_[:, 0:TT], ALU.add, ALU.mult), reads=[t1, k_], writes=[kr2])
                bv = f["bv"]
                c.op("pool", lambda e: e.tensor_tensor(bv[:, 0:TT], kk[:, 0:TT], a_[:, 0:TT], ALU.mult), reads=[kk, a_], writes=[bv])
                ca, cb = f["cwa"], f["cwb"]
                v3 = lambda t: t[:, 0:TT].rearrange("p (n c) -> p n c", c=C)
                src = lw
                s = 1
                dsts = [ca, cb]
                di = 0
                while s < C:
                    dst = dsts[di % 2]
                    di += 1
                    c.op("pool", lambda e, dst=dst, src=src, s=s: e.tensor_copy(v3(dst)[:, :, 0:s], v3(src)[:, :, 0:s]), reads=[src], writes=[dst])
                    c.op("dve", lambda e, dst=dst, src=src, s=s: e.tensor_tensor(v3(dst)[:, :, s:C], v3(src)[:, :, s:C], v3(src)[:, :, 0:C - s], ALU.add), reads=[src], writes=[dst])
                    src = dst
                    s *= 2
                cw = src
                E1, E2, E3, E4 = f["E1"], f["E2"], f["E3"], f["E4"]
                c.op("act", lambda e: e.activation(E1[:, 0:TT], cw[:, 0:TT], AF.Exp), reads=[cw], writes=[E1])
                c.op("act", lambda e: e.activation(E2[:, 0:TT], cw[:, 0:TT], AF.Exp, scale=-1.0), reads=[cw], writes=[E2])
                c.op("pool", lambda e: e.tensor_tensor(t1[:, 0:TT], cw[:, 0:TT], lw[:, 0:TT], ALU.subtract), reads=[cw, lw], writes=[t1])
                c.op("act", lambda e: e.activation(E3[:, 0:TT], t1[:, 0:TT], AF.Exp), reads=[t1], writes=[E3])
                for ci in range(nch):
                    c.op("act", lambda e, ci=ci: e.activation(E4[:, ci * C:(ci + 1) * C], cw[:, ci * C:(ci + 1) * C], AF.Exp, scale=-1.0, bias=cw[:, (ci + 1) * C - 1:(ci + 1) * C]), reads=[cw], writes=[E4])
                at, bt, kt, rt, Bh, Kh = f["at"], f["bt"], f["kt"], f["rt"], f["Bh"], f["Kh"]
                c.op("dve", lambda e: e.scalar_tensor_tensor(at[:, 0:TT], kk[:, 0:TT], -1.0, E3[:, 0:TT], ALU.mult, ALU.mult), reads=[kk, E3], writes=[at])
                c.op("pool", lambda e: e.tensor_tensor(bt[:, 0:TT], bv[:, 0:TT], E2[:, 0:TT], ALU.mult), reads=[bv, E2], writes=[bt])
                c.op("dve", lambda e: e.tensor_tensor(kt[:, 0:TT], kr2[:, 0:TT], E2[:, 0:TT], ALU.mult), reads=[kr2, E2], writes=[kt])
                c.op("pool", lambda e: e.tensor_tensor(rt[:, 0:TT], r_[:, 0:TT], E1[:, 0:TT], ALU.mult), reads=[r_, E1], writes=[rt])
                c.op("dve", lambda e: e.tensor_tensor(Bh[:, 0:TT], bv[:, 0:TT], E4[:, 0:TT], ALU.mult), reads=[bv, E4], writes=[Bh])
                c.op("pool", lambda e: e.tensor_tensor(Kh[:, 0:TT], kr2[:, 0:TT], E4[:, 0:TT], ALU.mult), reads=[kr2, E4], writes=[Kh])
            for j in range(2):
                do_pair(j)

            def do_chunk(ci):
                cs = slice(ci * C, (ci + 1) * C)
                tokc = t0 + ci * C
                cd = cslots[state["cslot"] % 2]
                state["cslot"] += 1
                for j in range(2):
                    js = slice(j * 128, (j + 1) * 128)
                    for nm, src in (("vtok", rw[4 + j]), ("Bhtok", F["Bh"][j]), ("Khtok", F["Kh"][j]), ("rtok", rw[j]), ("ktok", F["kr2"][j]), ("gtok", F["gate"][j])):
                        p = pb.get()
                        c.op("pe", lambda e, p=p, src=src: e.transpose(p[0:C, 0:128], src[:, cs], K[:, 128:256]), reads=[src, K.res], writes=[p])
                        dst = cd[nm]
                        if nm in ("vtok", "Khtok", "ktok"):
                            c.op("act", lambda e, p=p, dst=dst, js=js: e.activation(dst[0:C, js], p[0:C, 0:128], AF.Copy), reads=[p], writes=[dst])
                        else:
                            c.op("dve", lambda e, p=p, dst=dst, js=js: e.tensor_copy(dst[0:C, js], p[0:C, 0:128]), reads=[p], writes=[dst])

                def do_head(hh):
                    j, hb = hh // 2, (hh % 2) * 64
                    ps_ = slice(hb, hb + 64)
                    fs = slice(hh * 64, hh * 64 + 64)
                    d = slots[state["slot"] % NSLOT]
                    state["slot"] += 1
                    at, bt, kt, rt = F["at"][j], F["bt"][j], F["kt"][j], F["rt"][j]
                    mS = K[0:C, 512:512 + C]
                    mI = K[0:C, 384:384 + C]
                    M, Aak, Arb, Ark = d["M"], d["Aak"], d["Arb"], d["Ark"]
                    p = pb.get()
                    c.op("pe", lambda e, p=p: e.matmul(p[0:C, 0:C], bt[ps_, cs], at[ps_, cs], start=True, stop=True), reads=[bt, at], writes=[p])
                    c.op("dve", lambda e, p=p: e.scalar_tensor_tensor(M[0:C, 0:C], p[0:C, 0:C], -1.0, mS, ALU.mult, ALU.mult), reads=[p, K.res], writes=[M])
                    p = pb.get()
                    c.op("pe", lambda e, p=p: e.matmul(p[0:C, 0:C], kt[ps_, cs], at[ps_, cs], start=True, stop=True), reads=[kt, at], writes=[p])
                    c.op("dve", lambda e, p=p: e.tensor_tensor(Aak[0:C, 0:C], p[0:C, 0:C], mS, ALU.mult), reads=[p, K.res], writes=[Aak])
                    p = pb.get()
                    c.op("pe", lambda e, p=p: e.matmul(p[0:C, 0:C], bt[ps_, cs], rt[ps_, cs], start=True, stop=True), reads=[bt, rt], writes=[p])
                    c.op("dve", lambda e, p=p: e.tensor_tensor(Arb[0:C, 0:C], p[0:C, 0:C], mI, ALU.mult), reads=[p, K.res], writes=[Arb])
                    p = pb.get()
                    c.op("pe", lambda e, p=p: e.matmul(p[0:C, 0:C], kt[ps_, cs], rt[ps_, cs], start=True, stop=True), reads=[kt, rt], writes=[p])
                    c.op("dve", lambda e, p=p: e.tensor_tensor(Ark[0:C, 0:C], p[0:C, 0:C], mI, ALU.mult), reads=[p, K.res], writes=[Ark])
                    Rt = inv_unit_upper(c, pb, M, C, d, K)
                    vtok = cd["vtok"]
                    AV = d["AV"]
                    p = pb.get()
                    c.op("pe", lambda e, p=p: e.matmul(p[0:C, 0:64], Aak[0:C, 0:C], vtok[0:C, fs], start=True, stop=True), reads=[Aak, vtok], writes=[p])
                    c.op("act", lambda e, p=p: e.activation(AV[0:C, :], p[0:C, 0:64], AF.Copy), reads=[p], writes=[AV])
                    Told = T0[(j, cur[j])][hh % 2]
                    Tnew = T0[(j, 1 - cur[j])][hh % 2]
                    p1 = pb.get()
                    c.op("pe", lambda e: e.matmul(p1[0:C, 0:64], at[ps_, cs], Told.t[ps_, :], start=True, stop=True), reads=[at, Told], writes=[p1])
                    X = d["X"]
                    c.op("dve", lambda e: e.tensor_tensor(X[0:C, :], p1[0:C, 0:64], AV[0:C, :], ALU.add), reads=[p1, AV], writes=[X])
                    p2 = pb.get()
                    c.op("pe", lambda e: e.matmul(p2[0:C, 0:64], Rt[0:C, 0:C], X[0:C, :], start=True, stop=True), reads=[Rt, X], writes=[p2])
                    U = d["U"]
                    c.op("act", lambda e: e.activation(U[0:C, :], p2[0:C, 0:64], AF.Copy), reads=[p2], writes=[U])
                    p3 = pb.get()
                    c.op("pe", lambda e: e.matmul(p3[:, 0:64], cd["Bhtok"][0:C, j * 128:(j + 1) * 128], U[0:C, :], start=True, stop=False), reads=[cd["Bhtok"], U], writes=[p3], acc=True)
                    c.op("pe", lambda e: e.matmul(p3[:, 0:64], cd["Khtok"][0:C, j * 128:(j + 1) * 128], vtok[0:C, fs], start=False, stop=True), reads=[cd["Khtok"], vtok], writes=[p3], acc=True)
                    E1 = F["E1"][j]
                    c.op("dve", lambda e: e.scalar_tensor_tensor(Tnew.t[ps_, :], Told.t[ps_, :], E1[ps_, (ci + 1) * C - 1:(ci + 1) * C], p3[ps_, 0:64], ALU.mult, ALU.add), reads=[Told, E1, p3], writes=[Tnew])
                    p4 = pb.get()
                    c.op("pe", lambda e: e.matmul(p4[0:C, 0:64], Arb[0:C, 0:C], U[0:C, :], start=True, stop=False), reads=[Arb, U], writes=[p4], acc=True)
                    c.op("pe", lambda e: e.matmul(p4[0:C, 0:64], Ark[0:C, 0:C], vtok[0:C, fs], start=False, stop=True), reads=[Ark, vtok], writes=[p4], acc=True)
                    y2 = d["y2"]
                    c.op("act", lambda e: e.activation(y2[0:C, :], p4[0:C, 0:64], AF.Copy), reads=[p4], writes=[y2])
                    p5 = pb.get()
                    c.op("pe", lambda e: e.matmul(p5[0:C, 0:64], rt[ps_, cs], Told.t[ps_, :], start=True, stop=True), reads=[rt, Told], writes=[p5])
                    yc, st = d["yc"], d["st"]
                    c.op("dve", lambda e: e.tensor_tensor(yc[0:C, :], p5[0:C, 0:64], y2[0:C, :], ALU.add), reads=[p5, y2], writes=[yc])
                    c.op("dve", lambda e: e.reduce_sum(st[0:C, 0:1], yc[0:C, :], AX.X), reads=[yc], writes=[st])
                    c.op("dve", lambda e: e.tensor_scalar(st[0:C, 1:2], st[0:C, 0:1], -1.0 / 64, None, ALU.mult), reads=[st], writes=[st])
                    c.op("dve", lambda e: e.tensor_scalar(yc[0:C, :], yc[0:C, :], st[0:C, 1:2], None, ALU.add), reads=[yc, st], writes=[yc])
                    c.op("act", lambda e: e.activation(d["junk"][0:C, :], yc[0:C, :], AF.Square, accum_out=st[0:C, 2:3]), reads=[yc], writes=[d["junk"], st])
                    c.op("act", lambda e: e.activation(st[0:C, 3:4], st[0:C, 2:3], AF.Ln, bias=K[0:C, 3:4], scale=1.0 / 64), reads=[st, K.res], writes=[st])
                    c.op("act", lambda e: e.activation(st[0:C, 4:5], st[0:C, 3:4], AF.Exp, scale=-0.5), reads=[st], writes=[st])
                    c.op("dve", lambda e: e.scalar_tensor_tensor(cd["Y"][0:C, fs], yc[0:C, :], st[0:C, 4:5], lnw[0:C, fs], ALU.mult, ALU.mult), reads=[yc, st, lnw], writes=[cd["Y"]])
                for hh in range(4):
                    do_head(hh)
                rk, rks, o1, o2 = cd["rk"], cd["rks"], cd["o1"], cd["o2"]
                c.op("pool", lambda e: e.tensor_tensor(rk[0:C, :], cd["rtok"][0:C, :], cd["ktok"][0:C, :], ALU.mult), reads=[cd["rtok"], cd["ktok"]], writes=[rk])
                c.op("pool", lambda e: e.tensor_tensor(rk[0:C, :], rk[0:C, :], rkb[0:C, :], ALU.mult), reads=[rk, rkb], writes=[rk])
                c.op("dve", lambda e: e.reduce_sum(rks[0:C, :], rk[0:C, :].rearrange("p (h c) -> p h c", c=64), AX.X), reads=[rk], writes=[rks])
                c.op("pool", lambda e: e.tensor_tensor(o1[0:C, :], cd["Y"][0:C, :], lnb[0:C, :], ALU.add), reads=[cd["Y"], lnb], writes=[o1])
                for hh in range(4):
                    fs = slice(hh * 64, hh * 64 + 64)
                    c.op("dve", lambda e, fs=fs, hh=hh: e.scalar_tensor_tensor(o2[0:C, fs], cd["vtok"][0:C, fs], rks[0:C, hh:hh + 1], o1[0:C, fs], ALU.mult, ALU.add), reads=[cd["vtok"], rks, o1], writes=[o2])
                c.op("pool", lambda e: e.tensor_tensor(o1[0:C, :], o2[0:C, :], cd["gtok"][0:C, :], ALU.mult), reads=[o2, cd["gtok"]], writes=[o1])
                c.dma("sp", outs["mix"][tokc:tokc + C, :], o1[0:C, :], reads=[o1], writes=[outs["mix"]], semres=o1)
                cur[0] = 1 - cur[0]
                cur[1] = 1 - cur[1]
            for ci in range(nch):
                do_chunk(ci)
        for ti in range(ntile):
            do_tile(ti)
        for j in range(2):
            A, B = T0[(j, cur[j])]
            p = pb.get()
            c.op("pe", lambda e, p=p, A=A: e.transpose(p[0:64, 0:128], A.t[:, :], K[:, 128:256]), reads=[A, B, K.res], writes=[p])
            tmp = cslots[j]["o2"]
            c.op("dve", lambda e, p=p, tmp=tmp: e.tensor_copy(tmp[:, 0:128], p[0:64, 0:128]), reads=[p], writes=[tmp])
            c.dma("sp", outs["srw"][si, j], tmp[:, 0:128], reads=[tmp], writes=[outs["srw"]], semres=tmp)

    for si, (tok0, T, kind, b) in enumerate(seqs):
        do_seq(si, tok0, T, kind, b)


D = 4096
DFF = 11008
NM = DFF // 128


STOP = os.environ.get('FFN_STOP', '')
WQ = os.environ.get('WQ', 'sp')


def precast(c, srcs, nsplit=8):
    dummy = c.sb([128, 8], F32, name="castdummy")
    outs = []
    for s_ in srcs:
        shp = list(s_.t.shape)
        o = c.dram(s_.name + "_bf", shp, BF16)
        rows = shp[0]
        step = (rows + nsplit - 1) // nsplit
        for r0 in range(0, rows, step):
            r1 = min(rows, r0 + step)
            c.dma("pool", o[r0:r1, :], s_[r0:r1, :], reads=[s_], writes=[o], semres=dummy)
        outs.append(o)
    c.barrier(outs)
    return outs


def ffn_phase(c, K, mixT, xtok, wout, wg, wu, wd, gpostb, g2col, gpost2b, yout, tiles):
    RB = [c.sb([128, D], F32, name="RB%d" % i) for i in range(4)]
    h2T = c.sb([128, 32, 512], BF16, name="h2T")
    gb = c.sb([128, D], F32, name="gb")
    g2c = c.sb([128, 32], F32, name="g2c")
    c.dma("sp", g2c[:], g2col[:, :], reads=[g2col], writes=[g2c], semres=g2c)
    stt = c.sb([128, 64], F32, name="stt")
    pb = Banks(c, 6)
    pbb = Banks(c, 2, BF16, 1024)
    junk = c.sb([128, 1024], BF16, name="junkb")
    junkf = c.sb([128, 512], F32, name="junkf")
    slabs = [(0, 22), (22, 22), (44, 21), (65, 21)]

    def do_tile(tok0, T):
        nb = (T + 127) // 128
        bs = [min(128, T - 128 * i) for i in range(nb)]
        c.dma("sp", gb[:], gpostb[:, :], reads=[gpostb], writes=[gb], semres=gb)
        c.push()
        mt = c.sb([128, 32, 512], BF16, name="mt")
        wo = [c.sb([128, 8, 512], BF16, name="wo%d" % i) for i in range(2)]
        xs = [c.sb([128, 1024], F32, name="xs%d" % i) for i in range(2)]
        h2b = [c.sb([128, D], BF16, name="h2b%d" % i) for i in range(2)]
        c.dma(WQ, mt[:, :, 0:T], mixT[:, tok0:tok0 + T].rearrange("(k p) t -> p k t", p=128), reads=[mixT], writes=[mt], semres=mt)
        wi = 0
        for ct in range(8):
            ps = [pb.get() for _ in range(nb)]
            for kg in range(4):
                w_ = wo[wi % 2]
                wi += 1
                c.dma(WQ, w_[:], wout[kg * 1024:(kg + 1) * 1024, ct * 512:(ct + 1) * 512].rearrange("(k p) n -> p k n", p=128), reads=[wout], writes=[w_], semres=w_)
                for tb in range(nb):
                    for k in range(8):
                        kk = kg * 8 + k
                        c.op("pe", lambda e, p=ps[tb], tb=tb, k=k, kk=kk, w_=w_: e.matmul(p[0:bs[tb], :], mt[:, kk, tb * 128:tb * 128 + bs[tb]], w_[:, k, :], start=(kk == 0), stop=(kk == 31)),
                             reads=[mt, w_], writes=[ps[tb]], acc=True)
            for tb in range(nb):
                c.op("dve", lambda e, p=ps[tb], tb=tb, ct=ct: e.tensor_copy(RB[tb][0:bs[tb], ct * 512:(ct + 1) * 512], p[0:bs[tb], :]), reads=[ps[tb]], writes=[RB[tb]])
                if os.environ.get("NOSQ", "0") != "1":
                    if os.environ.get("SQACT", "0") == "1":
                        c.op("act", lambda e, p=ps[tb], tb=tb, ct=ct: e.activation(junkf[0:bs[tb], 0:512], RB[tb][0:bs[tb], ct * 512:(ct + 1) * 512], AF.Square, accum_out=stt[0:bs[tb], tb * 16 + ct:tb * 16 + ct + 1]), reads=[RB[tb]], writes=[junkf, stt])
                    else:
                        c.op("pool", lambda e, tb=tb, ct=ct: e.tensor_tensor(junkf[0:bs[tb], 0:512], RB[tb][0:bs[tb], ct * 512:(ct + 1) * 512], RB[tb][0:bs[tb], ct * 512:(ct + 1) * 512], ALU.mult), reads=[RB[tb]], writes=[junkf])
                        c.op("dve", lambda e, tb=tb, ct=ct: e.reduce_sum(stt[0:bs[tb], tb * 16 + ct:tb * 16 + ct + 1], junkf[0:bs[tb], 0:512], AX.X), reads=[junkf], writes=[stt])
        if STOP == 'B':
            for tb in range(nb):
                c.dma('sp', yout[tok0 + tb * 128:tok0 + tb * 128 + bs[tb], :], RB[tb][0:bs[tb], :], reads=[RB[tb]], writes=[yout], semres=RB[tb])
            c.pop()
            return
        xi = 0
        for tb in range(nb):
            n = bs[tb]
            r0 = tok0 + tb * 128
            c.op("dve", lambda e, tb=tb, n=n: e.reduce_sum(stt[0:n, tb * 16 + 8:tb * 16 + 9], stt[0:n, tb * 16 + 0:tb * 16 + 8], AX.X), reads=[stt], writes=[stt])
            c.op("act", lambda e, tb=tb, n=n: e.activation(stt[0:n, tb * 16 + 9:tb * 16 + 10], stt[0:n, tb * 16 + 8:tb * 16 + 9], AF.Ln, bias=K[0:n, 0:1], scale=1.0 / D), reads=[stt, K.res], writes=[stt])
            c.op("act", lambda e, tb=tb, n=n: e.activation(stt[0:n, tb * 16 + 10:tb * 16 + 11], stt[0:n, tb * 16 + 9:tb * 16 + 10], AF.Exp, scale=-0.5), reads=[stt], writes=[stt])
            c.op("dve", lambda e, tb=tb, n=n: e.scalar_tensor_tensor(RB[tb][0:n, :], RB[tb][0:n, :], stt[0:n, tb * 16 + 10:tb * 16 + 11], gb[0:n, :], ALU.mult, ALU.mult), reads=[RB[tb], stt, gb], writes=[RB[tb]])
            for q in range(4):
                x_ = xs[xi % 2]
                xi += 1
                c.dma("sp", x_[0:n, :], xtok[r0:r0 + n, q * 1024:(q + 1) * 1024], reads=[xtok], writes=[x_], semres=x_)
                c.op("pool", lambda e, tb=tb, n=n, q=q, x_=x_: e.tensor_tensor(RB[tb][0:n, q * 1024:(q + 1) * 1024], RB[tb][0:n, q * 1024:(q + 1) * 1024], x_[0:n, :], ALU.add), reads=[RB[tb], x_], writes=[RB[tb]])
            c.dma("sp", yout[r0:r0 + n, :], RB[tb][0:n, :], reads=[RB[tb]], writes=[yout], semres=RB[tb])
            hb = h2b[tb % 2]
            c.op("act", lambda e, tb=tb, n=n, hb=hb: e.activation(hb[0:n, :], RB[tb][0:n, :], AF.Square, accum_out=stt[0:n, tb * 16 + 11:tb * 16 + 12]), reads=[RB[tb]], writes=[hb, stt])
            c.op("act", lambda e, tb=tb, n=n: e.activation(stt[0:n, tb * 16 + 12:tb * 16 + 13], stt[0:n, tb * 16 + 11:tb * 16 + 12], AF.Ln, bias=K[0:n, 0:1], scale=1.0 / D), reads=[stt, K.res], writes=[stt])
            c.op("act", lambda e, tb=tb, n=n: e.activation(stt[0:n, tb * 16 + 13:tb * 16 + 14], stt[0:n, tb * 16 + 12:tb * 16 + 13], AF.Exp, scale=-0.5), reads=[stt], writes=[stt])
            c.op("dve", lambda e, tb=tb, n=n, hb=hb: e.tensor_scalar(hb[0:n, :], RB[tb][0:n, :], stt[0:n, tb * 16 + 13:tb * 16 + 14], None, ALU.mult), reads=[RB[tb], stt], writes=[hb])
            for g in range(4):
                p = pbb.get()
                for j in range(8):
                    k = g * 8 + j
                    c.op("pe", lambda e, p=p, j=j, k=k, n=n, hb=hb: e.transpose(p[:, j * 128:j * 128 + n], hb[0:n, k * 128:(k + 1) * 128], K.identb[0:n, 0:n]), reads=[hb, K.identb_res], writes=[p], acc=True)
                for j in range(8):
                    k = g * 8 + j
                    eng = "dve" if j % 2 == 0 else "pool"
                    if eng == "dve":
                        c.op("dve", lambda e, p=p, j=j, k=k, n=n, tb=tb: e.tensor_scalar(h2T[:, k, tb * 128:tb * 128 + n], p[:, j * 128:j * 128 + n], g2c[:, k:k + 1], None, ALU.mult), reads=[p, g2c], writes=[h2T])
                    else:
                        c.op("act", lambda e, p=p, j=j, k=k, n=n, tb=tb: e.activation(h2T[:, k, tb * 128:tb * 128 + n], p[:, j * 128:j * 128 + n], AF.Copy, scale=g2c[:, k:k + 1]), reads=[p, g2c], writes=[h2T])
        c.pop()
        if STOP == 'D':
            return
        c.dma("sp", gb[:], gpost2b[:, :], reads=[gpost2b], writes=[gb], semres=gb)
        c.push()
        hT = c.sb([128, 22, 512], BF16, name="hT")
        wgt = [c.sb([128, 32, 128], BF16, name="wg%d" % i) for i in range(2)]
        wut = [c.sb([128, 32, 128], BF16, name="wu%d" % i) for i in range(2)]
        wdt = [c.sb([128, 8, 512], BF16, name="wd%d" % i) for i in range(2)]
        sg = [c.sb([128, 512], F32, name="sg%d" % i) for i in range(2)]
        xs2 = [c.sb([128, 1024], F32, name="xs2_%d" % i) for i in range(2)]
        mi_ = 0
        di_ = 0
        for si, (m0, nm) in enumerate(slabs):
            for mi in range(nm):
                m = m0 + mi
                g_, u_ = wgt[mi_ % 2], wut[mi_ % 2]
                s_ = sg[mi_ % 2]
                mi_ += 1
                c.dma("sp", g_[:], wg[:, m * 128:(m + 1) * 128].rearrange("(k p) n -> p k n", p=128), reads=[wg], writes=[g_], semres=g_)
                c.dma("pool", u_[:], wu[:, m * 128:(m + 1) * 128].rearrange("(k p) n -> p k n", p=128), reads=[wu], writes=[u_], semres=u_)
                pg, pu_ = pb.get(), pb.get()
                for k in range(32):
                    c.op("pe", lambda e, pg=pg, k=k, g_=g_: e.matmul(pg[:, 0:T], g_[:, k, :], h2T[:, k, 0:T], start=(k == 0), stop=(k == 31)), reads=[g_, h2T], writes=[pg], acc=True)
                for k in range(32):
                    c.op("pe", lambda e, pu_=pu_, k=k, u_=u_: e.matmul(pu_[:, 0:T], u_[:, k, :], h2T[:, k, 0:T], start=(k == 0), stop=(k == 31)), reads=[u_, h2T], writes=[pu_], acc=True)
                c.op("act", lambda e, pg=pg, s_=s_: e.activation(s_[:, 0:T], pg[:, 0:T], AF.Silu), reads=[pg], writes=[s_])
                c.op("dve", lambda e, pu_=pu_, s_=s_, mi=mi: e.tensor_tensor(hT[:, mi, 0:T], s_[:, 0:T], pu_[:, 0:T], ALU.mult), reads=[pu_, s_], writes=[hT])
            ngrp = (nm + 7) // 8
            for ct in range(8):
                ps = [pb.get() for _ in range(nb)]
                for kg in range(ngrp):
                    k0 = kg * 8
                    nk = min(8, nm - k0)
                    w_ = wdt[di_ % 2]
                    di_ += 1
                    r0 = (m0 + k0) * 128
                    c.dma("sp", w_[:, 0:nk, :], wd[r0:r0 + nk * 128, ct * 512:(ct + 1) * 512].rearrange("(k p) n -> p k n", p=128), reads=[wd], writes=[w_], semres=w_)
                    for tb in range(nb):
                        for k in range(nk):
                            kk = k0 + k
                            c.op("pe", lambda e, p=ps[tb], tb=tb, k=k, kk=kk, w_=w_, nm=nm: e.matmul(p[0:bs[tb], :], hT[:, kk, tb * 128:tb * 128 + bs[tb]], w_[:, k, :], start=(kk == 0), stop=(kk == nm - 1)),
                                 reads=[hT, w_], writes=[ps[tb]], acc=True)
                for tb in range(nb):
                    if si == 0:
                        c.op("act", lambda e, p=ps[tb], tb=tb, ct=ct: e.activation(RB[tb][0:bs[tb], ct * 512:(ct + 1) * 512], p[0:bs[tb], :], AF.Copy), reads=[ps[tb]], writes=[RB[tb]])
                    else:
                        c.op("dve", lambda e, p=ps[tb], tb=tb, ct=ct: e.tensor_tensor(RB[tb][0:bs[tb], ct * 512:(ct + 1) * 512], RB[tb][0:bs[tb], ct * 512:(ct + 1) * 512], p[0:bs[tb], :], ALU.add), reads=[ps[tb], RB[tb]], writes=[RB[tb]])
        xi = 0
        for tb in range(nb):
            n = bs[tb]
            r0 = tok0 + tb * 128
            for q in range(4):
                c.op("act", lambda e, tb=tb, n=n, q=q: e.activation(junk[0:n, :], RB[tb][0:n, q * 1024:(q + 1) * 1024], AF.Square, accum_out=stt[0:n, tb * 16 + q:tb * 16 + q + 1]), reads=[RB[tb]], writes=[junk, stt])
            c.op("dve", lambda e, tb=tb, n=n: e.reduce_sum(stt[0:n, tb * 16 + 8:tb * 16 + 9], stt[0:n, tb * 16 + 0:tb * 16 + 4], AX.X), reads=[stt], writes=[stt])
            c.op("act", lambda e, tb=tb, n=n: e.activation(stt[0:n, tb * 16 + 9:tb * 16 + 10], stt[0:n, tb * 16 + 8:tb * 16 + 9], AF.Ln, bias=K[0:n, 0:1], scale=1.0 / D), reads=[stt, K.res], writes=[stt])
            c.op("act", lambda e, tb=tb, n=n: e.activation(stt[0:n, tb * 16 + 10:tb * 16 + 11], stt[0:n, tb * 16 + 9:tb * 16 + 10], AF.Exp, scale=-0.5), reads=[stt], writes=[stt])
            c.op("dve", lambda e, tb=tb, n=n: e.scalar_tensor_tensor(RB[tb][0:n, :], RB[tb][0:n, :], stt[0:n, tb * 16 + 10:tb * 16 + 11], gb[0:n, :], ALU.mult, ALU.mult), reads=[RB[tb], stt, gb], writes=[RB[tb]])
            for q in range(4):
                x_ = xs2[xi % 2]
                xi += 1
                c.dma("sp", x_[0:n, :], yout[r0:r0 + n, q * 1024:(q + 1) * 1024], reads=[yout], writes=[x_], semres=x_)
                c.op("pool", lambda e, tb=tb, n=n, q=q, x_=x_: e.tensor_tensor(RB[tb][0:n, q * 1024:(q + 1) * 1024], RB[tb][0:n, q * 1024:(q + 1) * 1024], x_[0:n, :], ALU.add), reads=[RB[tb], x_], writes=[RB[tb]])
            c.dma("sp", yout[r0:r0 + n, :], RB[tb][0:n, :], reads=[RB[tb]], writes=[yout], semres=RB[tb])
        c.pop()

    for (tok0, T) in tiles:
        do_tile(tok0, T)


from concourse.bass_utils import run_bass_kernel_spmd

SEQ = int(os.environ.get("KSEQ", "8192"))
NPROMPT = 2 * SEQ
NTOK = NPROMPT + 128
NT = NTOK // 128
SEQS = [(0, SEQ, "p", 0), (SEQ, SEQ, "p", 1)] + [(NPROMPT + 16 * b, 16, "s", b) for b in range(8)]
MCT = [(0, 128), (128, 128), (256, 128), (384, 128), (512, 128), (640, 128), (768, 96), (864, 96), (960, 128), (1088, 128)]
PSH = {"mu": [128, 10], "chp": [128, 2, 4], "w2": [96, 256], "a2": [96, 256], "g2": [128, 2, 256], "lnw": [128, 256],
       "lnb": [128, 256], "rkb": [128, 256], "shc": [128, 8, 10], "srw_in": [8, 2, 64, 128]}
NT2 = 2064
TILES2 = [(0, 512), (512, 512), (1024, 512), (1536, 512), (2048, 16)]


def build_mixer():
    nc = bass.Bass("TRN2", target_bir_lowering=False)
    c = Ctx(nc)
    EI = "ExternalInput"
    xall = c.dram("xall", [NTOK, 4096], F32, kind=EI)
    cst = c.dram("cst", [128, 1024], F32, kind=EI)
    wdn = c.dram("wdn", [4096, 1028], F32, kind=EI)
    wrw = c.dram("wrw", [4096, 1216], F32, kind=EI)
    gpre = c.dram("gpre", [128, 32], F32, kind=EI)
    cwd = c.dram("cwd", [128, 6, 4], F32, kind=EI)
    ccache = c.dram("ccache", [128, 8, 6, 3], F32, kind=EI)
    alog = c.dram("alog", [4, 1], F32, kind=EI)
    dtb = c.dram("dtb", [4, 1], F32, kind=EI)
    normw = c.dram("normw", [128, 128], F32, kind=EI)
    sdn_in = c.dram("sdn_in", [8, 2, 128, 128], F32, kind=EI)
    P = {k: c.dram(k, v, F32, kind=EI) for k, v in PSH.items()}
    hT = c.dram("hT", [NT, 128, 32, 128], BF16)
    EO = "ExternalOutput"
    o_dn = {"mix": c.dram("mixdn", [NTOK, 256], F32, kind=EO), "sdn": c.dram("sdn", [10, 2, 128, 128], F32, kind=EO),
            "conv": c.dram("conv", [10, 6, 128, 3], F32, kind=EO)}
    o_rw = {"mix": c.dram("mixrw", [NTOK, 256], F32, kind=EO), "srw": c.dram("srw", [10, 2, 64, 128], F32, kind=EO),
            "shift": c.dram("shift", [10, 1216], F32, kind=EO)}
    K = load_consts(c, cst)
    c.push(); build_hT(c, xall, hT, NT, K); c.pop()
    c.push(); dn_phase(c, K, hT, wdn, gpre, cwd, ccache, alog, dtb, normw, sdn_in, SEQS, o_dn, SEQ); c.pop()
    c.push(); rw_phase(c, K, hT, wrw, gpre, P, SEQS, o_rw); c.pop()
    c.final_wait("sp", list(o_dn.values()) + list(o_rw.values()))
    c.emit(); c.close()
    return nc


def build_ffn():
    nc = bass.Bass("TRN2", target_bir_lowering=False)
    c = Ctx(nc)
    EI = "ExternalInput"
    cst = c.dram("cst", [128, 1024], F32, kind=EI)
    mixT = c.dram("mixT", [4096, NT2], F32, kind=EI)
    xtok = c.dram("xtok", [NT2, 4096], F32, kind=EI)
    wout = c.dram("wout", [4096, 4096], F32, kind=EI)
    wg = c.dram("wg", [4096, DFF], F32, kind=EI)
    wu = c.dram("wu", [4096, DFF], F32, kind=EI)
    wd = c.dram("wd", [DFF, 4096], F32, kind=EI)
    gpostb = c.dram("gpostb", [128, 4096], F32, kind=EI)
    gpost2b = c.dram("gpost2b", [128, 4096], F32, kind=EI)
    g2col = c.dram("g2col", [128, 32], F32, kind=EI)
    yout = c.dram("yout", [NT2, 4096], F32, kind="ExternalOutput")
    K = load_consts(c, cst)
    mixT, wout, wg, wu, wd = precast(c, [mixT, wout, wg, wu, wd])
    c.push()
    ffn_phase(c, K, mixT, xtok, wout, wg, wu, wd, gpostb, g2col, gpost2b, yout, TILES2)
    c.pop()
    c.final_wait("sp", [yout])
    c.emit(); c.close()
    return nc


def _mch(v):
    out = np.zeros((128,) + v.shape[:-1] + (10,), np.float32)
    for m, (c0, rows) in enumerate(MCT):
        out[:rows, ..., m] = np.moveaxis(v[..., c0:c0 + rows], -1, 0)
    return out


def _f(a):
    return np.ascontiguousarray(a, dtype=np.float32)


def kernel(x_prompt, x_sample, state_dn, cache_dn_conv, state_rwkv, cache_rwkv_shift,
           g_mix_pre, g_mix_post, w_in, dn_conv_w, dn_a_log, dn_dt_bias, dn_norm_w,
           rw_mu, rw_w0, rw_w2, rw_a0, rw_a2, rw_g2, rw_k_k, rw_k_a, rw_r_k, rw_ln_w, rw_ln_b,
           w_out, g_ffn_pre, g_ffn_post, w_gate, w_up, w_down):
    A = lambda z: np.asarray(z, dtype=np.float32)
    xall = np.concatenate([A(x_prompt).reshape(NPROMPT, 4096), A(x_sample).reshape(128, 4096)], 0)
    w_in0 = A(w_in)[0]
    cst = make_consts_np()
    gpre = _f(A(g_mix_pre)[0].reshape(32, 128).T)
    o4 = 8224
    in_maps = []
    for c in range(8):
        hs = [2 * c, 2 * c + 1]
        dcols = []
        for base in (0, 2048, 4096, 6144):
            for h in hs:
                dcols.extend(range(base + h * 128, base + (h + 1) * 128))
        qkvcols = dcols[:768]
        dcols = dcols + [8192 + hs[0], 8192 + hs[1], 8208 + hs[0], 8208 + hs[1]]
        rcols = []
        for base in (0, 2048, 4096):
            rcols.extend(range(o4 + base + 256 * c, o4 + base + 256 * (c + 1)))
        rcols.extend(range(o4 + 6144, o4 + 6592))
        rsh = [r - o4 for r in rcols]
        ch = slice(256 * c, 256 * (c + 1))
        alog4 = np.zeros((4, 1), np.float32); alog4[2:, 0] = A(dn_a_log)[0, hs]
        dtb4 = np.zeros((4, 1), np.float32); dtb4[2:, 0] = A(dn_dt_bias)[0, hs]
        chp = np.stack([A(rw_w0)[0, ch], A(rw_a0)[0, ch], A(rw_k_k)[0, ch], A(rw_k_a)[0, ch]], -1).reshape(2, 128, 4).transpose(1, 0, 2)
        srw = A(state_rwkv)[0][:, 4 * c:4 * c + 4]
        m = {
            "xall": xall, "cst": cst, "gpre": gpre,
            "wdn": _f(w_in0[:, dcols]), "wrw": _f(w_in0[:, rcols]),
            "cwd": _f(A(dn_conv_w)[0][:, qkvcols].reshape(4, 6, 128).transpose(2, 1, 0)),
            "ccache": _f(A(cache_dn_conv)[0][:, :, qkvcols].reshape(8, 3, 6, 128).transpose(3, 0, 2, 1)),
            "alog": alog4, "dtb": dtb4, "normw": _f(np.tile(A(dn_norm_w)[0][None, :], (128, 1))),
            "sdn_in": _f(A(state_dn)[0][:, hs]),
            "mu": _mch(A(rw_mu)[0][rsh]), "chp": _f(chp),
            "w2": _f(A(rw_w2)[0][:, ch]), "a2": _f(A(rw_a2)[0][:, ch]),
            "g2": _f(A(rw_g2)[0][:, ch].reshape(2, 128, 256).transpose(1, 0, 2)),
            "lnw": _f(np.tile(A(rw_ln_w)[0, ch][None], (128, 1))), "lnb": _f(np.tile(A(rw_ln_b)[0, ch][None], (128, 1))),
            "rkb": _f(np.tile(A(rw_r_k)[0].reshape(-1)[ch][None], (128, 1))),
            "shc": _mch(A(cache_rwkv_shift)[0][:, 0][:, rsh]),
            "srw_in": _f(srw.reshape(8, 2, 2, 64, 64).transpose(0, 1, 3, 2, 4).reshape(8, 2, 64, 128)),
        }
        in_maps.append(m)
    nc1 = build_mixer()
    res1 = run_bass_kernel_spmd(nc1, in_maps, core_ids=list(range(8))).results
    mix = np.zeros((NTOK, 4096), np.float32)
    sdn = np.zeros((10, 16, 128, 128), np.float32)
    conv = np.zeros((10, 3, 6144), np.float32)
    srwo = np.zeros((10, 32, 64, 64), np.float32)
    shift = np.zeros((10, 6592), np.float32)
    for c in range(8):
        r = res1[c]
        hs = [2 * c, 2 * c + 1]
        mix[:, 256 * c:256 * (c + 1)] = r["mixdn"]
        mix[:, 2048 + 256 * c:2048 + 256 * (c + 1)] = r["mixrw"]
        sdn[:, hs] = r["sdn"]
        cv = np.asarray(r["conv"]).reshape(10, 6, 128, 3).transpose(0, 3, 1, 2)
        for gi, base in enumerate((0, 2048, 4096)):
            for hi, h in enumerate(hs):
                conv[:, :, base + h * 128: base + (h + 1) * 128] = cv[:, :, gi * 2 + hi]
        srwo[:, 4 * c:4 * c + 4] = np.asarray(r["srw"]).reshape(10, 2, 64, 2, 64).transpose(0, 1, 3, 2, 4).reshape(10, 4, 64, 64)
        sh = np.asarray(r["shift"])
        for gi, base in enumerate((0, 2048, 4096)):
            shift[:, base + 256 * c: base + 256 * (c + 1)] = sh[:, gi * 256:(gi + 1) * 256]
        if c == 0:
            shift[:, 6144:] = sh[:, 768:]
    gpostb = _f(np.tile(A(g_mix_post)[0][None], (128, 1)))
    gpost2b = _f(np.tile(A(g_ffn_post)[0][None], (128, 1)))
    g2col = _f(A(g_ffn_pre)[0].reshape(32, 128).T)
    wo, wg_, wu_, wd_ = A(w_out)[0], A(w_gate)[0], A(w_up)[0], A(w_down)[0]
    in2 = []
    for c in range(8):
        rows = np.r_[c * 2048:(c + 1) * 2048, NPROMPT + 16 * c:NPROMPT + 16 * (c + 1)]
        in2.append({"cst": cst, "mixT": _f(mix[rows].T), "xtok": _f(xall[rows]), "wout": wo, "wg": wg_, "wu": wu_, "wd": wd_,
                    "gpostb": gpostb, "gpost2b": gpost2b, "g2col": g2col})
    nc2 = build_ffn()
    res2 = run_bass_kernel_spmd(nc2, in2, core_ids=list(range(8))).results
    y = np.zeros((NTOK, 4096), np.float32)
    for c in range(8):
        rows = np.r_[c * 2048:(c + 1) * 2048, NPROMPT + 16 * c:NPROMPT + 16 * (c + 1)]
        y[rows] = res2[c]["yout"]
    y_prompt = y[:NPROMPT].reshape(2, SEQ, 4096)
    y_sample = y[NPROMPT:].reshape(8, 16, 4096)
    return (y_prompt, y_sample,
            sdn[None, 0:2], conv[None, 0:2], srwo[None, 0:2], shift[None, 0:2, None, :],
            sdn[None, 2:], conv[None, 2:], srwo[None, 2:], shift[None, 2:, None, :])
```
